# Optimizing a Trainium2 kernel written in Bass

```python
import jax, jax.numpy as jnp
from jax import lax
import numpy as np

D_MODEL = 2048
BATCH = 32
SEQ = 256
DEPTH = 2
DEC_BATCH = 4
DEC_SEQ = 2048
PAST_LEN = 256

GRID_W = 64
N_MIXERS = 4
MIX_W = D_MODEL
GROUP_W = MIX_W // N_MIXERS
N_HEADS = 4
HEAD_DV = GROUP_W // N_HEADS
GLA_DK = HEAD_DV // 2
GLA_RANK = 16
GLA_TAU = 16.0
LRU_BLOCKS = N_HEADS
LRU_BLOCK_W = GROUP_W // LRU_BLOCKS
LRU_C = 8.0
LRU_CONV = 4
RET_DK = HEAD_DV
MLSTM_DK = HEAD_DV
CHUNK = 64
N_DIR = 2
D_FF = ((8 * D_MODEL // 3 + 255) // 256) * 256
EPS = 1e-6

SPLIT_SIZES = (N_HEADS * GLA_DK, N_HEADS * GLA_DK, GROUP_W, GROUP_W, N_DIR * GLA_RANK,
               GROUP_W, GROUP_W,
               N_HEADS * RET_DK, N_HEADS * RET_DK, GROUP_W, GROUP_W,
               N_HEADS * MLSTM_DK, N_HEADS * MLSTM_DK, GROUP_W, GROUP_W,
               N_DIR * 2 * N_HEADS)
IN_W = sum(SPLIT_SIZES)
SPLIT_IDX = tuple(sum(SPLIT_SIZES[:i + 1]) for i in range(len(SPLIT_SIZES) - 1))

kernel_name = 'hybrid_bidir_gla_lru_ret_mlstm_diffusion_step'


def _rmsnorm(x, w):
    xf = x.astype(jnp.float32)
    y = xf * lax.rsqrt(jnp.mean(xf * xf, -1, keepdims=True) + EPS)
    return (y * w).astype(x.dtype)


def _headnorm(o, w):
    B, L, H, d = o.shape
    o = o * lax.rsqrt(jnp.mean(o * o, -1, keepdims=True) + EPS)
    return o.reshape(B, L, H * d) * w


def _dir(a, d):
    return a if d == 0 else jnp.flip(a, axis=1)


def _to_chunks(a):
    B, L, H, d = a.shape
    return a.reshape(B, L // CHUNK, CHUNK, H, d).transpose(1, 0, 3, 2, 4)


def _from_chunks(a):
    N, B, H, C, d = a.shape
    return a.transpose(1, 0, 3, 2, 4).reshape(B, N * C, H, d)


def _gla_scan(q, k, v, g, s0):
    mask = jnp.tril(jnp.ones((CHUNK, CHUNK), bool))

    def step(s, xs):
        qc, kc, vc, gc = xs
        b = jnp.cumsum(gc, axis=2)
        rel = jnp.where(mask[:, :, None], b[:, :, :, None, :] - b[:, :, None, :, :], -jnp.inf)
        scores = jnp.einsum('bhtd,bhsd,bhtsd->bhts', qc, kc, jnp.exp(rel))
        o = scores @ vc + jnp.einsum('bhtd,bhdv->bhtv', qc * jnp.exp(b), s)
        b_end = b[:, :, -1:, :]
        s_new = jnp.exp(b_end[:, :, 0, :, None]) * s + jnp.einsum('bhsd,bhsv->bhdv', kc * jnp.exp(b_end - b), vc)
        return s_new, o

    s_fin, o = lax.scan(step, s0, (_to_chunks(q), _to_chunks(k), _to_chunks(v), _to_chunks(g)))
    return _from_chunks(o), s_fin


def _ret_scan(q, k, v, log_gamma, s0):
    pos = jnp.arange(CHUNK, dtype=jnp.float32)
    diff = pos[:, None] - pos[None, :]
    lg = log_gamma[:, None, None]
    decay = jnp.where(diff >= 0, jnp.exp(jnp.maximum(diff, 0.0) * lg), 0.0)
    xi = jnp.exp((pos + 1.0) * log_gamma[:, None])
    zeta = jnp.exp((CHUNK - 1.0 - pos) * log_gamma[:, None])
    chunk_decay = jnp.exp(CHUNK * log_gamma)

    def step(s, xs):
        qc, kc, vc = xs
        o = jnp.einsum('bhts,bhsv->bhtv', jnp.einsum('bhtd,bhsd->bhts', qc, kc) * decay, vc) \
            + jnp.einsum('bhtd,bhdv->bhtv', qc, s) * xi[:, :, None]
        s_new = chunk_decay[:, None, None] * s + jnp.einsum('bhsd,bhsv->bhdv', kc * zeta[:, :, None], vc)
        return s_new, o

    s_fin, o = lax.scan(step, s0, (_to_chunks(q), _to_chunks(k), _to_chunks(v)))
    return _from_chunks(o), s_fin


def _mlstm_scan(q, k, v, log_i, log_f, c0, n0, m0):
    mask = jnp.tril(jnp.ones((CHUNK, CHUNK), bool))

    def step(carry, xs):
        c, n, m = carry
        qc, kc, vc, ic, fc = xs
        F = jnp.cumsum(fc, axis=-1)
        w_intra = jnp.where(mask, F[..., :, None] - F[..., None, :] + ic[..., None, :], -jnp.inf)
        w_inter = F + m[..., None]
        m_t = jnp.maximum(w_inter, jnp.max(w_intra, -1))
        p = jnp.exp(w_intra - m_t[..., None])
        a = jnp.exp(w_inter - m_t)
        qk = jnp.einsum('bhtd,bhsd->bhts', qc, kc) * p
        num = qk @ vc + a[..., None] * jnp.einsum('bhtd,bhdv->bhtv', qc, c)
        den = jnp.sum(qk, -1) + a * jnp.einsum('bhtd,bhd->bht', qc, n)
        h = num / jnp.maximum(jnp.abs(den), jnp.exp(-m_t))[..., None]
        w_end = F[..., -1:] - F + ic
        w_end_inter = F[..., -1] + m
        m_new = jnp.maximum(w_end_inter, jnp.max(w_end, -1))
        a_end = jnp.exp(w_end_inter - m_new)
        p_end = jnp.exp(w_end - m_new[..., None])
        c_new = a_end[..., None, None] * c + jnp.einsum('bhs,bhsd,bhsv->bhdv', p_end, kc, vc)
        n_new = a_end[..., None] * n + jnp.einsum('bhs,bhsd->bhd', p_end, kc)
        return (c_new, n_new, m_new), h

    xs = (_to_chunks(q), _to_chunks(k), _to_chunks(v),
          _to_chunks(log_i[..., None])[..., 0], _to_chunks(log_f[..., None])[..., 0])
    (c_f, n_f, m_f), h = lax.scan(step, (c0, n0, m0), xs)
    return _from_chunks(h), c_f, n_f, m_f


def _linear_scan(log_a, u, h0):
    u = u.at[:, 0].add(jnp.exp(log_a[:, 0]) * h0)

    def combine(e1, e2):
        la1, u1 = e1
        la2, u2 = e2
        return la1 + la2, jnp.exp(la2) * u1 + u2

    _, h = lax.associative_scan(combine, (log_a, u), axis=1)
    return h, h[:, -1]


def _gla_mixer(q, k, v, g_out, lr, gate_up, gate_b, norm_w, s0):
    B, L, _ = q.shape
    qh = q.reshape(B, L, N_HEADS, GLA_DK) * GLA_DK ** -0.5
    kh = k.reshape(B, L, N_HEADS, GLA_DK)
    vh = v.reshape(B, L, N_HEADS, HEAD_DV)
    lr = lr.reshape(B, L, N_DIR, GLA_RANK)
    out, finals = 0.0, []
    for d in range(N_DIR):
        g = jax.nn.log_sigmoid(lr[:, :, d] @ gate_up[d] + gate_b[d]) / GLA_TAU
        g = g.reshape(B, L, N_HEADS, GLA_DK).astype(jnp.float32)
        o, s = _gla_scan(_dir(qh, d), _dir(kh, d), _dir(vh, d), _dir(g, d), s0[:, d].astype(jnp.float32))
        out = out + _dir(o, d)
        finals.append(s)
    return _headnorm(out, norm_w) * jax.nn.silu(g_out), jnp.stack(finals, 1)


def _lru_mixer(xb, gb, conv_w, conv_b, gate_w, gate_b, lam, h0):
    B, L, _ = xb.shape
    pad_l = LRU_CONV // 2
    xp = jnp.pad(xb, ((0, 0), (pad_l, LRU_CONV - 1 - pad_l), (0, 0)))
    xc = sum(xp[:, j:j + L] * conv_w[j] for j in range(LRU_CONV)) + conv_b
    out, finals = 0.0, []
    for d in range(N_DIR):
        xd = _dir(xc, d)
        gates = jnp.einsum('blnc,gnce->gblne', xd.reshape(B, L, LRU_BLOCKS, LRU_BLOCK_W), gate_w[d])
        gates = gates.reshape(2, B, L, GROUP_W) + gate_b[d][:, None, None, :]
        r = jax.nn.sigmoid(gates[0])
        i = jax.nn.sigmoid(gates[1])
        log_a = (-LRU_C * r * jax.nn.softplus(-lam[d])).astype(jnp.float32)
        u = (jnp.sqrt(-jnp.expm1(2.0 * log_a)) * (i * xd)).astype(jnp.float32)
        hs, hf = _linear_scan(log_a, u, h0[:, d].astype(jnp.float32))
        out = out + _dir(hs, d)
        finals.append(hf)
    return out * jax.nn.gelu(gb), jnp.stack(finals, 1)


def _ret_mixer(q, k, v, g_out, decay_logit, norm_w, s0):
    B, L, _ = q.shape
    qh = q.reshape(B, L, N_HEADS, RET_DK)
    kh = k.reshape(B, L, N_HEADS, RET_DK) * RET_DK ** -0.5
    vh = v.reshape(B, L, N_HEADS, HEAD_DV)
    out, finals = 0.0, []
    for d in range(N_DIR):
        log_gamma = jax.nn.log_sigmoid(decay_logit[d].astype(jnp.float32))
        o, s = _ret_scan(_dir(qh, d), _dir(kh, d), _dir(vh, d), log_gamma, s0[:, d].astype(jnp.float32))
        out = out + _dir(o, d)
        finals.append(s)
    return _headnorm(out, norm_w) * jax.nn.silu(g_out), jnp.stack(finals, 1)


def _mlstm_mixer(q, k, v, o_pre, if_pre, gate_b, norm_w, c0, n0, m0):
    B, L, _ = q.shape
    qh = q.reshape(B, L, N_HEADS, MLSTM_DK)
    kh = k.reshape(B, L, N_HEADS, MLSTM_DK) * MLSTM_DK ** -0.5
    vh = v.reshape(B, L, N_HEADS, HEAD_DV)
    gp = (if_pre.reshape(B, L, N_DIR, 2, N_HEADS) + gate_b).astype(jnp.float32)
    out, fc, fn, fm = 0.0, [], [], []
    for d in range(N_DIR):
        log_i = gp[:, :, d, 0]
        log_f = jax.nn.log_sigmoid(gp[:, :, d, 1])
        h, c_f, n_f, m_f = _mlstm_scan(_dir(qh, d), _dir(kh, d), _dir(vh, d), _dir(log_i, d), _dir(log_f, d),
                                       c0[:, d].astype(jnp.float32), n0[:, d].astype(jnp.float32),
                                       m0[:, d].astype(jnp.float32))
        out = out + _dir(h, d)
        fc.append(c_f)
        fn.append(n_f)
        fm.append(m_f)
    y = _headnorm(out, norm_w) * jax.nn.sigmoid(o_pre)
    return y, jnp.stack(fc, 1), jnp.stack(fn, 1), jnp.stack(fm, 1)


def _conv_ffn(h, w_up, conv_w, conv_b, w_down, rows):
    B, L, _ = h.shape
    u, g = jnp.split(h @ w_up, 2, axis=-1)
    g = g.reshape(B, rows, L // rows, D_FF)
    g = lax.conv_general_dilated(g, conv_w[:, :, None, :], (1, 1), 'SAME',
                                 dimension_numbers=('NHWC', 'HWIO', 'NHWC'), feature_group_count=D_FF)
    g = g.reshape(B, L, D_FF) + conv_b
    return (jax.nn.silu(g) * u) @ w_down


def _layer(x, cond, l, P, st, rows):
    mod = jax.nn.silu(cond) @ P['w_mod'][l] + P['b_mod'][l]
    shift1, scale1, gate1, shift2, scale2, gate2 = jnp.split(mod[:, None, :], 6, axis=-1)
    h = _rmsnorm(x, P['norm1_w'][l]) * (1 + scale1) + shift1
    proj = (h @ P['w_in'][l]).astype(jnp.float32)
    (a_q, a_k, a_v, a_g, a_lr, b_x, b_g, c_q, c_k, c_v, c_g,
     d_q, d_k, d_v, d_o, d_if) = jnp.split(proj, SPLIT_IDX, axis=-1)
    s_gla, s_lru, s_ret, s_mc, s_mn, s_mm = st
    y_a, f_gla = _gla_mixer(a_q, a_k, a_v, a_g, a_lr, P['gla_gate_up'][l], P['gla_gate_b'][l],
                            P['gla_norm_w'][l], s_gla)
    y_b, f_lru = _lru_mixer(b_x, b_g, P['lru_conv_w'][l], P['lru_conv_b'][l], P['lru_gate_w'][l],
                            P['lru_gate_b'][l], P['lru_lambda'][l], s_lru)
    y_c, f_ret = _ret_mixer(c_q, c_k, c_v, c_g, P['ret_decay_logit'][l], P['ret_norm_w'][l], s_ret)
    y_d, f_mc, f_mn, f_mm = _mlstm_mixer(d_q, d_k, d_v, d_o, d_if, P['mlstm_gate_b'][l], P['mlstm_norm_w'][l],
                                         s_mc, s_mn, s_mm)
    mix = jnp.concatenate([y_a, y_b, y_c, y_d], axis=-1).astype(x.dtype)
    x = x + gate1 * (mix @ P['w_out'][l])
    h2 = _rmsnorm(x, P['norm2_w'][l]) * (1 + scale2) + shift2
    x = x + gate2 * _conv_ffn(h2, P['w_up'][l], P['ffn_conv_w'][l], P['ffn_conv_b'][l], P['w_down'][l], rows)
    return x, (f_gla, f_lru, f_ret, f_mc, f_mn, f_mm)


def _trunk(x, cond, P, states, rows):
    finals = []
    for l in range(DEPTH):
        x, f = _layer(x, cond, l, P, tuple(s[:, l] for s in states), rows)
        finals.append(f)
    y = _rmsnorm(x, P['final_norm_w'])
    new_state = tuple(jnp.stack([f[i] for f in finals], axis=1) for i in range(len(states)))
    return y, new_state


def _zero_states(B):
    f32 = jnp.float32
    return (jnp.zeros((B, DEPTH, N_DIR, N_HEADS, GLA_DK, HEAD_DV), f32),
            jnp.zeros((B, DEPTH, N_DIR, GROUP_W), f32),
            jnp.zeros((B, DEPTH, N_DIR, N_HEADS, RET_DK, HEAD_DV), f32),
            jnp.zeros((B, DEPTH, N_DIR, N_HEADS, MLSTM_DK, HEAD_DV), f32),
            jnp.zeros((B, DEPTH, N_DIR, N_HEADS, MLSTM_DK), f32),
            jnp.zeros((B, DEPTH, N_DIR, N_HEADS), f32))


def setup_inputs(seed: int = 0) -> dict:
    key = jax.random.key(seed)
    ks = jax.random.split(key, 40)
    f32 = jnp.float32

    def nrm(k, shape, scale):
        return jax.random.normal(k, shape, f32) * scale

    lam_u = jax.random.uniform(ks[20], (DEPTH, N_DIR, GROUP_W), f32, minval=0.9, maxval=0.999)
    lam_s = lam_u ** (1.0 / LRU_C)
    ret_base = jnp.log(jnp.exp2(jnp.arange(N_HEADS, dtype=f32) + 5.0) - 1.0)
    mlstm_ib = nrm(ks[24], (DEPTH, N_DIR, N_HEADS), 0.1)
    mlstm_fb = jnp.linspace(3.0, 6.0, N_HEADS, dtype=f32) + nrm(ks[25], (DEPTH, N_DIR, N_HEADS), 0.1)
    return {
        'x_prompt': nrm(ks[0], (BATCH, SEQ, D_MODEL), 1.0),
        'x_sample': nrm(ks[1], (DEC_BATCH, DEC_SEQ, D_MODEL), 1.0),
        'c': nrm(ks[2], (DEC_BATCH, D_MODEL), 1.0),
        'state_gla': nrm(ks[3], (DEC_BATCH, DEPTH, N_DIR, N_HEADS, GLA_DK, HEAD_DV), 0.3),
        'state_lru': nrm(ks[4], (DEC_BATCH, DEPTH, N_DIR, GROUP_W), 0.5),
        'state_ret': nrm(ks[5], (DEC_BATCH, DEPTH, N_DIR, N_HEADS, RET_DK, HEAD_DV), 0.3),
        'state_mlstm_c': nrm(ks[6], (DEC_BATCH, DEPTH, N_DIR, N_HEADS, MLSTM_DK, HEAD_DV), 0.3),
        'state_mlstm_n': nrm(ks[7], (DEC_BATCH, DEPTH, N_DIR, N_HEADS, MLSTM_DK), 0.3),
        'state_mlstm_m': nrm(ks[8], (DEC_BATCH, DEPTH, N_DIR, N_HEADS), 0.5),
        'c_ctx': nrm(ks[9], (D_MODEL,), 1.0),
        'w_mod': nrm(ks[10], (DEPTH, D_MODEL, 6 * D_MODEL), 0.5 * D_MODEL ** -0.5),
        'b_mod': nrm(ks[11], (DEPTH, 6 * D_MODEL), 0.02),
        'norm1_w': 1.0 + nrm(ks[12], (DEPTH, D_MODEL), 0.02),
        'w_in': nrm(ks[13], (DEPTH, D_MODEL, IN_W), D_MODEL ** -0.5),
        'gla_gate_up': nrm(ks[14], (DEPTH, N_DIR, GLA_RANK, N_HEADS * GLA_DK), GLA_RANK ** -0.5),
        'gla_gate_b': nrm(ks[15], (DEPTH, N_DIR, N_HEADS * GLA_DK), 0.1),
        'gla_norm_w': 1.0 + nrm(ks[16], (DEPTH, GROUP_W), 0.02),
        'lru_conv_w': nrm(ks[17], (DEPTH, LRU_CONV, GROUP_W), LRU_CONV ** -0.5),
        'lru_conv_b': nrm(ks[18], (DEPTH, GROUP_W), 0.02),
        'lru_gate_w': nrm(ks[19], (DEPTH, N_DIR, 2, LRU_BLOCKS, LRU_BLOCK_W, LRU_BLOCK_W), LRU_BLOCK_W ** -0.5),
        'lru_gate_b': nrm(ks[21], (DEPTH, N_DIR, 2, GROUP_W), 0.1),
        'lru_lambda': jnp.log(lam_s) - jnp.log1p(-lam_s),
        'ret_decay_logit': ret_base + nrm(ks[22], (DEPTH, N_DIR, N_HEADS), 0.1),
        'ret_norm_w': 1.0 + nrm(ks[23], (DEPTH, GROUP_W), 0.02),
        'mlstm_gate_b': jnp.stack([mlstm_ib, mlstm_fb], axis=2),
        'mlstm_norm_w': 1.0 + nrm(ks[26], (DEPTH, GROUP_W), 0.02),
        'w_out': nrm(ks[27], (DEPTH, MIX_W, D_MODEL), MIX_W ** -0.5),
        'norm2_w': 1.0 + nrm(ks[28], (DEPTH, D_MODEL), 0.02),
        'w_up': nrm(ks[29], (DEPTH, D_MODEL, 2 * D_FF), D_MODEL ** -0.5),
        'ffn_conv_w': nrm(ks[30], (DEPTH, 3, 3, D_FF), 1.0 / 3.0),
        'ffn_conv_b': nrm(ks[31], (DEPTH, D_FF), 0.02),
        'w_down': nrm(ks[32], (DEPTH, D_FF, D_MODEL), D_FF ** -0.5),
        'final_norm_w': 1.0 + nrm(ks[33], (D_MODEL,), 0.02),
    }


def reference(x_prompt, x_sample, c, state_gla, state_lru, state_ret, state_mlstm_c, state_mlstm_n,
              state_mlstm_m, c_ctx, w_mod, b_mod, norm1_w, w_in, gla_gate_up, gla_gate_b, gla_norm_w,
              lru_conv_w, lru_conv_b, lru_gate_w, lru_gate_b, lru_lambda, ret_decay_logit, ret_norm_w,
              mlstm_gate_b, mlstm_norm_w, w_out, norm2_w, w_up, ffn_conv_w, ffn_conv_b, w_down, final_norm_w):
    P = dict(w_mod=w_mod, b_mod=b_mod, norm1_w=norm1_w, w_in=w_in, gla_gate_up=gla_gate_up,
             gla_gate_b=gla_gate_b, gla_norm_w=gla_norm_w, lru_conv_w=lru_conv_w, lru_conv_b=lru_conv_b,
             lru_gate_w=lru_gate_w, lru_gate_b=lru_gate_b, lru_lambda=lru_lambda,
             ret_decay_logit=ret_decay_logit, ret_norm_w=ret_norm_w, mlstm_gate_b=mlstm_gate_b,
             mlstm_norm_w=mlstm_norm_w, w_out=w_out, norm2_w=norm2_w, w_up=w_up, ffn_conv_w=ffn_conv_w,
             ffn_conv_b=ffn_conv_b, w_down=w_down, final_norm_w=final_norm_w)
    y_prompt, (n_gla, n_lru, n_ret, n_mc, n_mn, n_mm) = _trunk(
        x_prompt, c_ctx[None, :], P, _zero_states(x_prompt.shape[0]), 1)
    rows = x_sample.shape[1] // GRID_W
    y_sample, _ = _trunk(x_sample, c, P,
                         (state_gla, state_lru, state_ret, state_mlstm_c, state_mlstm_n, state_mlstm_m), rows)
    return (y_prompt, y_sample, n_gla, n_lru, n_ret, n_mc, n_mn, n_mm)
```

```python
import numpy as np
from contextlib import ExitStack
import concourse.bass as bass
import concourse.mybir as mybir
from concourse.bass_utils import run_bass_kernel_spmd

F32 = mybir.dt.float32
BF16 = mybir.dt.bfloat16
AF = mybir.ActivationFunctionType
ALU = mybir.AluOpType
AX = mybir.AxisListType
ENGS = ['pe', 'act', 'dve', 'pool', 'sp']
EPOCH = 12000

T = 2048
D = 2048
KC = 16
NT = 16
DFF = 5632
FC = 44
INW = 6704
DEPTH = 2
EPS = 1e-6
OFF = dict(a_q=0, a_k=256, a_v=512, a_g=1024, a_lr=1536, b_x=1568, b_g=2080, c_q=2592, c_k=3104, c_v=3616,
           c_g=4128, d_q=4640, d_k=5152, d_v=5664, d_o=6176, d_if=6688)
NCONST = 20


class Op:
    __slots__ = ('eng', 'fn', 'deps', 'tok', 'sig', 'dma_key', 'val', 'idx')


class Sched:
    def __init__(self):
        self.q = {e: [] for e in ENGS}
        self.lastw = {}
        self.readers = {}
        self.dcount = {}
        self.dma_since = []
        self.out_dmas = []
        self.n = 0

    def _new(self, eng, fn, dma_key):
        o = Op()
        o.eng = eng; o.fn = fn; o.deps = {}; o.tok = None; o.sig = False
        o.dma_key = dma_key; o.val = 0; o.idx = self.n
        self.n += 1
        return o

    def op(self, eng, fn, r=(), w=(), dma_key=None, is_out=False):
        o = self._new(eng, fn, dma_key)
        for k in r:
            lw = self.lastw.get(k)
            if lw is not None:
                o.deps[lw] = 'raw'
        for k in w:
            lw = self.lastw.get(k)
            if lw is not None and lw not in o.deps:
                o.deps[lw] = 'waw'
            for rd in self.readers.get(k, ()):
                if rd not in o.deps:
                    o.deps[rd] = 'war'
        for k in r:
            lst = self.readers.setdefault(k, [])
            if dma_key is None:
                lst[:] = [x for x in lst if not (x.dma_key is None and x.eng == eng)]
            lst.append(o)
        for k in w:
            self.lastw[k] = o
            self.readers[k] = []
        o.deps.pop(o, None)
        if dma_key is not None:
            self.dcount[dma_key] = self.dcount.get(dma_key, 0) + 16
            o.val = self.dcount[dma_key]
            self.dma_since.append(o)
            if is_out:
                self.out_dmas.append(o)
        self.q[eng].append(o)
        return o

    def barrier(self):
        deps = {}
        for e in ENGS:
            for o in reversed(self.q[e]):
                if o.fn is not None and o.dma_key is None:
                    deps[o] = 'raw'
                    break
        for o in self.dma_since:
            deps[o] = 'raw'
        for e in ENGS:
            b = self._new(e, None, None)
            b.deps = dict(deps)
            self.q[e].append(b)
        self.dma_since = []
        self.lastw = {}
        self.readers = {}

    def finish(self):
        b = self._new('sp', None, None)
        b.deps = {o: 'raw' for o in self.out_dmas}
        self.q['sp'].append(b)

    def keep(self, o, dep, kind):
        if dep.dma_key is not None or o.dma_key is not None:
            return True
        if dep.eng != o.eng:
            return True
        if o.eng == 'pe':
            return False
        return kind == 'raw'

    def emit(self, nc, stack):
        for e in ENGS:
            for o in self.q[e]:
                for d, kind in o.deps.items():
                    if self.keep(o, d, kind):
                        d.sig = True
        esems = {}
        for e in ENGS:
            ns = sum(1 for o in self.q[e] if o.sig and o.dma_key is None)
            esems[e] = [stack.enter_context(nc.semaphore(f"s_{e}_{i}")) for i in range(ns // EPOCH + 1)]
        dsems = {}
        for i, k in enumerate(self.dcount):
            dsems[k] = stack.enter_context(nc.semaphore(f"d_{i}"))
        for e in ENGS:
            c = 0
            for o in self.q[e]:
                if o.dma_key is not None:
                    o.tok = (dsems[o.dma_key], o.val)
                elif o.sig:
                    o.tok = (esems[e][c // EPOCH], c % EPOCH + 1)
                    c += 1
        sched = self

        def replay(e, eng):
            waited = {}
            for o in sched.q[e]:
                for d in sorted(o.deps, key=lambda x: x.idx):
                    if not sched.keep(o, d, o.deps[d]):
                        continue
                    sem, val = d.tok
                    key = id(sem)
                    if waited.get(key, 0) < val:
                        eng.wait_ge(sem, val)
                        waited[key] = val
                if o.fn is None:
                    continue
                ins = o.fn(eng)
                if o.tok is not None:
                    ins.then_inc(o.tok[0], 16 if o.dma_key is not None else 1)

        with nc.Block() as block:
            @block.tensor
            def _(eng):
                replay('pe', eng)

            @block.scalar
            def _(eng):
                replay('act', eng)

            @block.vector
            def _(eng):
                replay('dve', eng)

            @block.gpsimd
            def _(eng):
                replay('pool', eng)

            @block.sync
            def _(eng):
                replay('sp', eng)


class Arena:
    def __init__(self, ap, nwords):
        self.ap = ap; self.n = nwords; self.off = 0; self.hi = 0

    def alloc(self, free_shape, dtype):
        nel = int(np.prod(free_shape))
        nw = nel if dtype == F32 else (nel + 1) // 2
        nw = (nw + 7) // 8 * 8
        assert self.off + nw <= self.n, f"arena overflow {self.off}+{nw}>{self.n}"
        a = self.ap[:, self.off:self.off + nw]
        self.off += nw; self.hi = max(self.hi, self.off)
        if dtype != F32:
            a = a.bitcast(dtype)
        a = a[:, 0:nel]
        if len(free_shape) == 2:
            a = a.rearrange("p (a b) -> p a b", a=free_shape[0])
        elif len(free_shape) == 3:
            a = a.rearrange("p (a b c) -> p a b c", a=free_shape[0], b=free_shape[1])
        return a

    def mark(self):
        return self.off

    def release(self, m):
        self.off = m


def make_consts():
    p = np.arange(128)
    blk = (p[:, None] // 64) == (p[None, :] // 64)
    c = np.zeros((128, NCONST, 128), np.float32)
    c[:, 0, :] = np.eye(128)
    c[:, 1, :] = 1.0
    c[:, 2, :] = blk & (p[:, None] <= p[None, :])
    c[:, 3, :] = blk & (p[:, None] >= p[None, :])
    c[:, 4, :] = blk & (p[:, None] > p[None, :])
    c[:, 5, :] = blk & (p[:, None] < p[None, :])
    c[:, 6, :] = (p[:, None] < 64)
    c[:, 7, :] = (p[:, None] >= 64)
    c[:, 8, :] = blk
    for i in range(4):
        c[:, 9 + i, :] = c[:, 2 + i, :] * (-1.0 / 16.0)
    c[:, 13, 0] = p % 64 + 1
    c[:, 13, 1] = 64 - p % 64
    c[:, 13, 2] = 63 - p % 64
    c[:, 13, 3] = p % 64
    for h in range(4):
        c[h, 14 + h, :] = 1.0
    c[:, 18, 0:4] = (p[:, None] % 8) == np.arange(4)[None, :]
    c[:, 18, 4:8] = (p[:, None] % 8) == (4 + np.arange(4))[None, :]
    c[:, 18, 16:32] = (p[:, None] // 8) == np.arange(16)[None, :]
    return c


def fm(v, n):
    v = np.asarray(v, np.float32)
    lead = v.shape[:-1]
    return np.ascontiguousarray(np.moveaxis(v.reshape(lead + (n, 128)), -1, 0))


def bc(v):
    v = np.asarray(v, np.float32).reshape(-1)
    return np.ascontiguousarray(np.broadcast_to(v[None, :], (128, v.size)))


def core_masks(seq_len, grid_w, keep):
    t = np.arange(T)
    pos = t % seq_len
    m = np.ones((7, T), np.float32)
    m[0] = pos <= seq_len - 3
    m[1] = pos <= seq_len - 2
    m[2] = pos >= 1
    m[3] = np.where((pos == 0) & (t > 0), keep, 1.0)
    m[4] = np.where((pos == seq_len - 1) & (t < T - 1), keep, 1.0)
    col = t % grid_w
    m[5] = col != grid_w - 1
    m[6] = col != 0
    return m


def build(stop_after=None, dumps=()):
    nc = bass.Bass("TRN2", target_bir_lowering=False)

    def din(name, shape, dt=F32):
        return nc.dram_tensor(name, list(shape), dt, kind="ExternalInput").ap()

    def dout(name, shape, dt=F32):
        return nc.dram_tensor(name, list(shape), dt, kind="ExternalOutput").ap()

    def dscr(name, shape, dt=F32):
        return nc.dram_tensor(name, list(shape), dt, kind="Internal").ap()

    X = din("x", [T, D])
    CONDT = din("condT", [128, KC])
    KEEP = din("keep", [128, 1])
    MASKS = din("masks", [7, T])
    CONSTS = din("consts", [128, NCONST, 128])
    W_MOD = din("w_mod", [DEPTH, D, 6 * D])
    B_MODT = din("b_modT", [128, DEPTH, 96])
    NORM1T = din("norm1T", [128, DEPTH, KC])
    NORM2T = din("norm2T", [128, DEPTH, KC])
    FNORMT = din("fnormT", [128, KC])
    W_IN = din("w_in", [DEPTH, D, INW])
    W_OUT = din("w_out", [DEPTH, D, D])
    W_UP = din("w_up", [DEPTH, D, 2 * DFF])
    W_DOWN = din("w_down", [DEPTH, DFF, D])
    GLA_UPB = din("gla_upb", [DEPTH, 2, 64, 256])
    GLA_NBC = din("gla_nbc", [128, DEPTH, 512])
    S_GLA0 = din("s_gla0", [DEPTH, 2, 4, 64, 128])
    LRU_CWT = din("lru_cwT", [128, DEPTH, 4, 4])
    LRU_CBT = din("lru_cbT", [128, DEPTH, 4])
    LRU_GW = din("lru_gw", [DEPTH, 2, 2, 4, 128, 128])
    LRU_GBT = din("lru_gbT", [128, DEPTH, 2, 2, 4])
    LRU_LAMT = din("lru_lamT", [128, DEPTH, 2, 4])
    S_LRU0T = din("s_lru0T", [128, DEPTH, 2, 4])
    RET_DLBC = din("ret_dlbc", [128, DEPTH * 2 * 4])
    RET_NBC = din("ret_nbc", [128, DEPTH, 512])
    S_RET0 = din("s_ret0", [DEPTH, 2, 4, 128, 128])
    ML_GBBC = din("ml_gbbc", [128, DEPTH, 16])
    ML_NBC = din("ml_nbc", [128, DEPTH, 512])
    S_MC0 = din("s_mc0", [DEPTH, 2, 4, 128, 128])
    S_MN0T = din("s_mn0T", [128, DEPTH, 2, 4])
    S_MM0BC = din("s_mm0bc", [128, DEPTH * 2 * 4])
    FFN_CWT = din("ffn_cwT", [128, DEPTH, 9, FC])
    FFN_CBT = din("ffn_cbT", [128, DEPTH, FC])

    Y = dout("y", [T, D])
    O_GLA = dout("o_gla", [8, DEPTH, 2, 4, 64, 128])
    O_LRU = dout("o_lru", [8, DEPTH, 2, 512])
    O_RET = dout("o_ret", [8, DEPTH, 2, 4, 128, 128])
    O_MC = dout("o_mc", [8, DEPTH, 2, 4, 128, 128])
    O_MN = dout("o_mn", [8, DEPTH, 2, 4, 128])
    O_MM = dout("o_mm", [8, DEPTH, 2, 4])

    XTA = dscr("xta", [D, T])
    XTB = dscr("xtb", [D, T])
    MIXT = dscr("mixt", [D, T], BF16)
    MODROW = dscr("modrow", [DEPTH, 6 * D])
    ACTS = dscr("acts", [DFF, T], BF16)
    DUMP = {}
    if 'hT' in dumps:
        DUMP['hT'] = dout("dump_hT", [D, T], BF16)
    if 'mixT' in dumps:
        DUMP['mixT'] = dout("dump_mixT", [D, T], BF16)
    if 'xtb' in dumps:
        DUMP['xtb'] = dout("dump_xtb", [D, T])
    if 'xta' in dumps:
        DUMP['xta'] = dout("dump_xta", [D, T])
    if 'mod' in dumps:
        DUMP['mod'] = dout("dump_mod", [128, 192])

    S = Sched()
    st = ExitStack()
    NW = 53000
    arena_t = st.enter_context(nc.sbuf_tensor("arena", [128, NW], F32))
    ps_t = st.enter_context(nc.psum_tensor("ps", [128, 8 * 512], F32))
    A = Arena(arena_t[:, :], NW)
    PSA = ps_t[:, :]
    PS = [PSA[:, b * 512:(b + 1) * 512] for b in range(8)]

    def MM(out, lhsT, rhs, start, stop, r, w):
        return S.op('pe', lambda e: e.matmul(out, lhsT=lhsT, rhs=rhs, start=start, stop=stop), r=r, w=w)

    def TR(out, in_, ident, r, w):
        return S.op('pe', lambda e: e.transpose(out, in_, ident), r=r, w=w)

    def ACT(out, in_, func, r, w, bias=None, scale=None, accum=None):
        kw = {}
        if bias is not None:
            kw['bias'] = bias
        if scale is not None:
            kw['scale'] = scale
        if accum is not None:
            kw['accum_out'] = accum
        return S.op('act', lambda e: e.activation(out=out, in_=in_, func=func, **kw), r=r, w=w)

    def TT(out, a, b, op, r, w, eng='dve'):
        return S.op(eng, lambda e: e.tensor_tensor(out=out, in0=a, in1=b, op=op), r=r, w=w)

    def TS(out, a, s1, op0, r, w, s2=None, op1=None):
        if op1 is None:
            return S.op('dve', lambda e: e.tensor_scalar(out=out, in0=a, scalar1=s1, scalar2=None, op0=op0), r=r, w=w)
        return S.op('dve', lambda e: e.tensor_scalar(out=out, in0=a, scalar1=s1, scalar2=s2, op0=op0, op1=op1), r=r, w=w)

    def STT(out, a, s, b, op0, op1, r, w):
        return S.op('dve', lambda e: e.scalar_tensor_tensor(out=out, in0=a, scalar=s, in1=b, op0=op0, op1=op1), r=r, w=w)

    def CPV(out, in_, r, w):
        return S.op('dve', lambda e: e.tensor_copy(out=out, in_=in_), r=r, w=w)

    def RECIP(out, in_, r, w):
        return S.op('dve', lambda e: e.reciprocal(out=out, in_=in_), r=r, w=w)

    def MEMSET(out, v, w, eng='dve'):
        return S.op(eng, lambda e: e.memset(out, v), w=w)

    def DMA(eng, out, in_, r, w, key, is_out=False, slow=False):
        if slow:
            return S.op(eng, lambda e: e.dma_start(out=out, in_=in_, allow_slow_non_contiguous=True), r=r, w=w, dma_key=key, is_out=is_out)
        return S.op(eng, lambda e: e.dma_start(out=out, in_=in_), r=r, w=w, dma_key=key, is_out=is_out)

    cpc = [0]

    def CP(out, in_, r, w):
        cpc[0] += 1
        if cpc[0] % 2:
            return ACT(out, in_, AF.Copy, r, w)
        return CPV(out, in_, r, w)

    cst = A.alloc([NCONST, 128], F32)
    identb = A.alloc([128], BF16)
    onesb = A.alloc([128], BF16)
    modT = A.alloc([DEPTH * 96], F32)
    AB = A.alloc([DEPTH, 6, KC], F32)
    fnT = A.alloc([KC], F32)
    keepc = A.alloc([1], F32)
    n1T = A.alloc([DEPTH, KC], F32)
    n2T = A.alloc([DEPTH, KC], F32)
    epsc = A.alloc([1], F32)
    DMA('sp', cst, CONSTS, [], ['cst'], 'cst')
    DMA('sp', fnT, FNORMT, [], ['fnT'], 'fnT')
    DMA('sp', keepc, KEEP, [], ['keepc'], 'keepc')
    DMA('sp', n1T, NORM1T, [], ['n1T'], 'n1T')
    DMA('sp', n2T, NORM2T, [], ['n2T'], 'n2T')
    CPV(identb, cst[:, 0, :], ['cst'], ['identb'])
    CPV(onesb, cst[:, 1, :], ['cst'], ['onesb'])
    MEMSET(epsc, EPS, ['epsc'])
    ident = cst[:, 0, :]
    hT_off = A.off
    hT = A.alloc([KC, T], BF16)
    hT_raw = arena_t[:, hT_off:hT_off + 16384]
    base_mark = A.mark()

    def XTv(XT):
        return XT.rearrange("(kc p) t -> p kc t", p=128)

    scb = A.alloc([KC], BF16)
    bmt = A.alloc([DEPTH * 96], F32)
    mrow = [A.alloc([256], F32) for _ in range(2)]
    mod_state = dict(next=0, mr=0)
    NMODG = DEPTH * 48

    def mod_groups(n):
        for _ in range(n):
            gi = mod_state['next']
            if gi >= NMODG:
                return
            mod_state['next'] += 1
            l, g = divmod(gi, 48)
            wt, wkey = load_w(W_MOD[l], g * 256, 256)
            b = pctr[0] % 4; pctr[0] += 1
            for kc in range(KC):
                MM(PS[b][0:1, 0:256], scb[:, kc:kc + 1], wt[:, kc, 0:256], kc == 0, kc == KC - 1, [wkey, 'scb'], [('ps', b)])
            r = mod_state['mr'] % 2; mod_state['mr'] += 1
            ACT(mrow[r][0:1, :], PS[b][0:1, 0:256], AF.Copy, [('ps', b)], [('mrow', r)])
            DMA('sp', MODROW[l:l + 1, g * 256:(g + 1) * 256], mrow[r][0:1, :], [('mrow', r)], [('MODROW', gi)], ('mrow', r))

    def mod_finalize(l, parts):
        for part in parts:
            c0 = l * 96 + part * 16
            DMA('sp', modT[:, c0:c0 + 16], MODROW[l, part * 2048:(part + 1) * 2048].rearrange("(j p) -> p j", p=128),
                [('MODROW', l * 48 + part * 8 + i) for i in range(8)], [('modT', l, part)], ('modT', l, part), slow=True)
            TT(modT[:, c0:c0 + 16], modT[:, c0:c0 + 16], bmt[:, c0:c0 + 16], ALU.add, [('modT', l, part), 'bmt'], [('modT', l, part)])
        mo = modT[:, l * 96:(l + 1) * 96]
        if 1 in parts:
            STT(AB[:, l, 0, :], mo[:, 16:32], 1.0, n1T[:, l, :], ALU.add, ALU.mult, [('modT', l, 1), 'n1T'], ['AB'])
        if 0 in parts:
            CPV(AB[:, l, 1, :], mo[:, 0:16], [('modT', l, 0)], ['AB'])
        if 2 in parts:
            CPV(AB[:, l, 2, :], mo[:, 32:48], [('modT', l, 2)], ['AB'])
        if 4 in parts:
            STT(AB[:, l, 3, :], mo[:, 64:80], 1.0, n2T[:, l, :], ALU.add, ALU.mult, [('modT', l, 4), 'n2T'], ['AB'])
        if 3 in parts:
            CPV(AB[:, l, 4, :], mo[:, 48:64], [('modT', l, 3)], ['AB'])
        if 5 in parts:
            CPV(AB[:, l, 5, :], mo[:, 80:96], [('modT', l, 5)], ['AB'])

    def phase_prologue():
        m = A.mark()
        xin = [A.alloc([D], F32) for _ in range(2)]
        xo = [A.alloc([KC, 128], F32) for _ in range(2)]
        condt = A.alloc([KC], F32)
        wslots.clear()
        wslots.extend(A.alloc([KC, 256], BF16) for _ in range(3))
        DMA('sp', condt, CONDT, [], ['condt'], 'condt')
        DMA('sp', bmt, B_MODT.rearrange("p l j -> p (l j)"), [], ['bmt'], 'bmt')
        ACT(scb, condt, AF.Silu, ['condt'], ['scb'])
        xtv = XTv(XTA)
        nb = 0
        for tt in range(NT):
            s = tt % 2
            DMA('sp', xin[s], X[tt * 128:(tt + 1) * 128, :], [], [('xin', s)], ('xin', s))
            for g in range(4):
                b = 4 + nb % 4; nb += 1
                for j in range(4):
                    kc = g * 4 + j
                    TR(PS[b][:, j * 128:(j + 1) * 128], xin[s][:, kc * 128:(kc + 1) * 128], ident,
                       [('xin', s), 'cst'], [('ps', b)])
                CP(xo[s][:, g * 4:(g + 1) * 4, :], PS[b].rearrange("p (a b) -> p a b", a=4), [('ps', b)], [('xo', s)])
            DMA('sp', xtv[:, :, tt * 128:(tt + 1) * 128], xo[s], [('xo', s)], [('XTA', tt // 4)], ('xo', s))
            mod_groups(1)
        mod_finalize(0, [0, 1])
        S.barrier()
        A.release(m)

    def phase_norm(XT, acol, bcol, final=False):
        m = A.mark()
        xt = [A.alloc([KC, 512], F32) for _ in range(2)]
        sq = A.alloc([KC, 512], BF16)
        s1 = A.alloc([512], F32)
        rstd = A.alloc([512], F32)
        tmp = [A.alloc([512], F32) for _ in range(2)]
        if final:
            yT = [hT_raw[:, i * 8192:(i + 1) * 8192].rearrange("p (a b) -> p a b", a=KC) for i in range(2)]
            yo = [A.alloc([D], F32) for _ in range(2)]
        xtv = XTv(XT)
        nb = 0
        for tq in range(4):
            s = tq % 2
            DMA('sp', xt[s], xtv[:, :, tq * 512:(tq + 1) * 512], [('XT', tq)], [('xt', s)], ('xt', s))
            ACT(sq, xt[s], AF.Square, [('xt', s)], ['sq'])
            for kc in range(KC):
                MM(PS[4], onesb, sq[:, kc, :], kc == 0, kc == KC - 1, ['onesb', 'sq'], [('ps', 4)])
            ACT(s1, PS[4], AF.Sqrt, [('ps', 4), 'epsc'], ['s1'], bias=epsc[:, 0:1], scale=1.0 / D)
            RECIP(rstd, s1, ['s1'], ['rstd'])
            for kc in range(KC):
                ts_ = kc % 2
                TT(tmp[ts_], xt[s][:, kc, :], rstd, ALU.mult, [('xt', s), 'rstd'], [('tmp', ts_)])
                if final:
                    dst = yT[s][:, kc, :]
                    wkey = ('yT', s)
                else:
                    dst = hT[:, kc, tq * 512:(tq + 1) * 512]
                    wkey = ('hT', tq)
                if bcol is None:
                    ACT(dst, tmp[ts_], AF.Identity, [('tmp', ts_), 'AB', 'fnT'], [wkey], scale=acol[:, kc:kc + 1])
                else:
                    ACT(dst, tmp[ts_], AF.Identity, [('tmp', ts_), 'AB'], [wkey], bias=bcol[:, kc:kc + 1], scale=acol[:, kc:kc + 1])
            if final:
                for t4 in range(4):
                    tt = tq * 4 + t4
                    ys = tt % 2
                    for g in range(4):
                        b = nb % 4; nb += 1
                        for j in range(4):
                            kc = g * 4 + j
                            TR(PS[b][:, j * 128:(j + 1) * 128], yT[s][:, kc, t4 * 128:(t4 + 1) * 128], ident,
                               [('yT', s), 'cst'], [('ps', b)])
                        CP(yo[ys][:, g * 512:(g + 1) * 512], PS[b], [('ps', b)], [('yo', ys)])
                    DMA('sp', Y[tt * 128:(tt + 1) * 128, :], yo[ys], [('yo', ys)], [], ('yo', ys), is_out=True)
        S.barrier()
        A.release(m)

    wslots = []
    wctr = [0]
    pctr = [0]

    def load_w(W2d, col0, ncols, nk=KC):
        i = wctr[0] % len(wslots); wctr[0] += 1
        wt = wslots[i]
        DMA('pool', wt[:, 0:nk, 0:ncols], W2d[:, col0:col0 + ncols].rearrange("(kc p) n -> p kc n", p=128),
            [], [('w', i)], ('w', i))
        return wt, ('w', i)

    def proj_fm_g(W2d, col0, M, evac, sub=0, wt=None, wkey=None):
        if wt is None:
            wt, wkey = load_w(W2d, col0, M)
        for tq in range(4):
            b = pctr[0] % 4; pctr[0] += 1
            for kc in range(KC):
                MM(PS[b][0:M, :], wt[:, kc, sub:sub + M], hT[:, kc, tq * 512:(tq + 1) * 512], kc == 0, kc == KC - 1,
                   [wkey, ('hT', tq)], [('ps', b)])
                if kc % 4 == 3 and kc != KC - 1:
                    yield
            evac(tq, PS[b][0:M, :], ('ps', b))
            yield

    def proj_tm_g(W2d, col0, N, evac):
        wt, wkey = load_w(W2d, col0, N)
        for tt in range(NT):
            b = pctr[0] % 4; pctr[0] += 1
            for kc in range(KC):
                MM(PS[b][:, 0:N], hT[:, kc, tt * 128:(tt + 1) * 128], wt[:, kc, 0:N], kc == 0, kc == KC - 1,
                   [wkey, ('hT', tt // 4)], [('ps', b)])
                if kc % 8 == 7 and kc != KC - 1:
                    yield
            evac(tt, PS[b][:, 0:N], ('ps', b))
            yield

    def proj_fm(*a, **k):
        for _ in proj_fm_g(*a, **k):
            pass

    def proj_tm(*a, **k):
        for _ in proj_tm_g(*a, **k):
            pass

    def scan_ws():
        return dict(S32=A.alloc([2, 136], F32), S16=A.alloc([2, 136], BF16), PT=[A.alloc([64], BF16) for _ in range(4)],
                    dd=[A.alloc([2], F32) for _ in range(4)], sq=A.alloc([NT, 128], F32), ss=A.alloc([NT], F32),
                    rs=A.alloc([NT], F32), pi=[0])

    def scan_core(ws, dk, base, AT, QT, V, nv, KE, pmask, rowscale, oscale, emF, decay, init_state, out_state, Oacc, rkeys,
                  full_k=False, skew=0, bg=None):
        rows = slice(base, base + dk)
        krows = slice(0, 128) if full_k else rows
        S32, S16, PT, dd = ws['S32'], ws['S16'], ws['PT'], ws['dd']
        for d in range(2):
            init_state(d, S32[rows, d, 0:nv], ('S32', d))
            CP(S16[rows, d, 0:nv], S32[rows, d, 0:nv], [('S32', d)], [('S16', d)])
        steps = [(stp, d) for stp in range(32) for d in range(2)]
        NSTEP = len(steps)
        written = set()

        def geo(i):
            stp, d = steps[i]
            c = stp if d == 0 else 31 - stp
            tt, hb = c // 2, (c % 2) * 64
            return stp, d, c, tt, slice(hb, hb + 64), slice(c * 64, (c + 1) * 64)

        pi0 = ws['pi'][0]
        ws['pi'][0] += NSTEP

        def slots(i):
            pi = pi0 + i
            return pi % 4, pi % 4, pi % 2

        def psum_aps(i):
            stp, d, c, tt, hs, tok = geo(i)
            pslot, oslot, sslot = slots(i)
            pP = PS[4][hs, pslot * 64:(pslot + 1) * 64]
            pO = PS[5 + oslot // 2][hs, (oslot % 2) * 256:(oslot % 2) * 256 + nv]
            pS = PS[7][rows, sslot * 256:sslot * 256 + nv]
            return pP, pO, pS, ('psP', pslot), ('psO', oslot), ('psS', sslot)

        def emit_P(i):
            stp, d, c, tt, hs, tok = geo(i)
            pP, pO, pS, kP, kO, kS = psum_aps(i)
            MM(pP, AT[d][krows, tok], QT[d][krows, tok], True, True, rkeys, [kP])

        def emit_M(i):
            stp, d, c, tt, hs, tok = geo(i)
            pslot = slots(i)[0]
            pP, pO, pS, kP, kO, kS = psum_aps(i)
            if rowscale is None:
                TT(PT[pslot][hs, :], pP, pmask[d][hs, hs], ALU.mult, [kP] + rkeys, [('PT', pslot)])
            else:
                STT(PT[pslot][hs, :], pP, rowscale[d][hs, tt:tt + 1], pmask[d][hs, hs], ALU.mult, ALU.mult,
                    [kP] + rkeys, [('PT', pslot)])

        def emit_O(i):
            stp, d, c, tt, hs, tok = geo(i)
            pslot = slots(i)[0]
            pP, pO, pS, kP, kO, kS = psum_aps(i)
            MM(pO, PT[pslot][hs, :], V[hs, tt, 0:nv], True, False, [('PT', pslot)] + rkeys, [kO])
            MM(pO, QT[d][krows, tok], S16[krows, d, 0:nv], False, True, rkeys + [('S16', d)], [kO])

        def emit_E(i):
            stp, d, c, tt, hs, tok = geo(i)
            oslot = slots(i)[1]
            pP, pO, pS, kP, kO, kS = psum_aps(i)
            oa = Oacc[hs, tt, :]
            oak = ('O', c)
            first = c not in written
            written.add(c)
            if emF is not None:
                dsl = dd[oslot][hs, :]
                ACT(dsl[:, 0:1], pO[:, 128:129], AF.Abs, [kO], [('dd', oslot)])
                TS(dsl[:, 0:1], dsl[:, 0:1], emF[d][hs, tt:tt + 1], ALU.max, [('dd', oslot)] + rkeys, [('dd', oslot)])
                RECIP(dsl[:, 1:2], dsl[:, 0:1], [('dd', oslot)], [('dd', oslot)])
                sc = dsl[:, 1:2]
                sck = [('dd', oslot)]
            elif oscale is not None:
                sc = oscale[d][hs, 0:1]
                sck = []
            else:
                sc = None
                sck = []
            if sc is None:
                if first:
                    CPV(oa, pO[:, 0:128], [kO], [oak])
                else:
                    TT(oa, pO[:, 0:128], oa, ALU.add, [kO, oak], [oak])
            else:
                if first:
                    TS(oa, pO[:, 0:128], sc, ALU.mult, [kO] + sck + rkeys, [oak])
                else:
                    STT(oa, pO[:, 0:128], sc, oa, ALU.mult, ALU.add, [kO, oak] + sck + rkeys, [oak])

        def emit_S(i):
            stp, d, c, tt, hs, tok = geo(i)
            pP, pO, pS, kP, kO, kS = psum_aps(i)
            MM(pS, KE[d][hs, tt, 0:dk], V[hs, tt, 0:nv], True, True, rkeys, [kS])

        def emit_U(i):
            stp, d, c, tt, hs, tok = geo(i)
            pP, pO, pS, kP, kO, kS = psum_aps(i)
            STT(S32[rows, d, 0:nv], S32[rows, d, 0:nv], decay(d, c), pS, ALU.mult, ALU.add,
                [('S32', d), kS] + rkeys, [('S32', d)])
            seg_end = (c % 4 == 3) if d == 0 else (c % 4 == 0)
            if seg_end:
                out_state(d, c // 4, S32[rows, d, 0:nv], ('S32', d))
                if stp != 31:
                    TS(S32[rows, d, 0:nv], S32[rows, d, 0:nv], keepc[rows, 0:1], ALU.mult,
                       [('S32', d), 'keepc'], [('S32', d)])
            if stp != 31:
                ACT(S16[rows, d, 0:nv], S32[rows, d, 0:nv], AF.Copy, [('S32', d)], [('S16', d)])

        skew = getattr(build, 'scan_skew', skew)
        if skew == 0:
            for i in range(NSTEP):
                emit_P(i); emit_M(i)
                if bg is not None:
                    bg()
                emit_O(i); emit_E(i)
                if bg is not None:
                    bg()
                emit_S(i); emit_U(i)
                if bg is not None:
                    bg()
        elif skew == 2:
            emit_P(0)
            emit_M(0)
            emit_S(0)
            for i in range(NSTEP):
                if i + 1 < NSTEP:
                    emit_P(i + 1)
                emit_O(i)
                emit_E(i)
                emit_U(i)
                if i + 1 < NSTEP:
                    emit_M(i + 1)
                    emit_S(i + 1)
        elif skew == 4:
            emit_P(0)
            emit_M(0)
            emit_S(0)
            for i in range(NSTEP):
                if i + 1 < NSTEP:
                    emit_P(i + 1)
                emit_O(i)
                if i + 1 < NSTEP:
                    emit_M(i + 1)
                emit_E(i)
                emit_U(i)
                if i + 1 < NSTEP:
                    emit_S(i + 1)
        elif skew == 3:
            emit_S(0)
            for i in range(NSTEP):
                emit_P(i); emit_M(i); emit_O(i)
                if i + 1 < NSTEP:
                    emit_S(i + 1)
                emit_E(i); emit_U(i)
        else:
            emit_P(0)
            emit_M(0)
            emit_S(0)
            for i in range(NSTEP):
                if i + 1 < NSTEP:
                    emit_P(i + 1)
                emit_O(i)
                if i + 1 < NSTEP:
                    emit_M(i + 1)
                emit_U(i)
                emit_E(i)
                if i + 1 < NSTEP:
                    emit_S(i + 1)

    def head_finish(ws, Oacc, gate, gkey, nbc_ap, row0, ybuf, yT):
        sq, ss, rs = ws['sq'], ws['ss'], ws['rs']
        okeys = [('O', c) for c in range(32)]
        TT(sq, Oacc, Oacc, ALU.mult, okeys, ['hsq'])
        S.op('dve', lambda e: e.tensor_reduce(out=ss, in_=sq, axis=AX.X, op=ALU.add), r=['hsq'], w=['hss'])
        ACT(ss, ss, AF.Sqrt, ['hss', 'epsc'], ['hss'], bias=epsc[:, 0:1], scale=1.0 / 128.0)
        RECIP(rs, ss, ['hss'], ['hrs'])
        TT(sq, Oacc, rs.unsqueeze(2).to_broadcast([128, NT, 128]), ALU.mult, okeys + ['hrs'], ['hsq'])
        TT(sq, sq, nbc_ap.unsqueeze(1).to_broadcast([128, NT, 128]), ALU.mult, ['hsq', 'nbc'], ['hsq'])
        TT(ybuf, sq, gate, ALU.mult, ['hsq', gkey], ['ybuf'])
        for g in range(4):
            b = pctr[0] % 4; pctr[0] += 1
            pb = PS[b].bitcast(BF16)
            for j in range(4):
                tt = g * 4 + j
                TR(pb[:, j * 128:(j + 1) * 128], ybuf[:, tt, :], identb, ['ybuf', 'identb'], [('ps', b)])
            CP(yT[:, g * 512:(g + 1) * 512], pb[:, 0:512], [('ps', b)], ['yT'])
        DMA('sp', MIXT[row0:row0 + 128, :], yT, ['yT'], [('MIXT', row0 // 128)], 'yT')

    def mixer_lru(l):
        m = A.mark()
        W = W_IN[l]
        msk = A.alloc([5, T], BF16)
        DMA('pool', msk, MASKS[0:5, :].partition_broadcast(128), [], ['msk'], 'msk')
        cw = A.alloc([4, 4], F32); cb = A.alloc([4], F32); gb = A.alloc([2, 2, 4], F32); lam = A.alloc([2, 4], F32)
        h0 = A.alloc([2, 4], F32)
        DMA('sp', cw, LRU_CWT[:, l], [], ['cw'], 'cw')
        DMA('sp', cb, LRU_CBT[:, l], [], ['cb'], 'cb')
        DMA('sp', gb, LRU_GBT[:, l], [], ['gb'], 'gb')
        DMA('sp', lam, LRU_LAMT[:, l], [], ['lam'], 'lam')
        DMA('sp', h0, S_LRU0T[:, l], [], ['h0'], 'h0')
        nsp = A.alloc([2, 4], F32)
        ACT(nsp, lam, AF.Exp, ['lam'], ['nsp'], scale=-1.0)
        ACT(nsp, nsp, AF.Ln, ['nsp'], ['nsp'], bias=1.0)
        TS(nsp, nsp, -8.0, ALU.mult, ['nsp'], ['nsp'])
        gw = A.alloc([16, 128], BF16)
        DMA('pool', gw, LRU_GW[l].rearrange("d g n c e -> c (d g n) e"), [], ['gw'], 'gw')
        B0 = A.alloc([T], F32); B1 = A.alloc([T], F32); xc = A.alloc([T], F32); B3 = A.alloc([T], F32)
        i_ = A.alloc([T], F32); hs0 = A.alloc([T], F32); hs1 = A.alloc([T], F32); gg = A.alloc([T], F32)
        xcb = A.alloc([T], BF16); yb = A.alloc([T], BF16)
        ho = A.alloc([2, 8], F32)
        xb, a_, xm, u_, r_, g3 = B0, B0, B1, B1, B3, B3
        for n in range(4):
            if l == 0:
                mod_groups(5)

            def ev_x(tq, ps, pk):
                CP(xb[:, tq * 512:(tq + 1) * 512], ps, [pk], ['B0'])
            proj_fm(W, OFF['b_x'] + n * 128, 128, ev_x)

            def ev_g(tq, ps, pk):
                CP(gg[:, tq * 512:(tq + 1) * 512], ps, [pk], ['gg'])
            proj_fm(W, OFF['b_g'] + n * 128, 128, ev_g)
            TS(xc, xb, cw[:, 2, n:n + 1], ALU.mult, ['B0', 'cw', 'cb'], ['xc'], s2=cb[:, n:n + 1], op1=ALU.add)
            TT(xm, xb, msk[:, 0, :], ALU.mult, ['B0', 'msk'], ['B1'])
            STT(xc[:, 2:T], xm[:, 0:T - 2], cw[:, 0, n:n + 1], xc[:, 2:T], ALU.mult, ALU.add, ['B1', 'xc', 'cw'], ['xc'])
            TT(xm, xb, msk[:, 1, :], ALU.mult, ['B0', 'msk'], ['B1'])
            STT(xc[:, 1:T], xm[:, 0:T - 1], cw[:, 1, n:n + 1], xc[:, 1:T], ALU.mult, ALU.add, ['B1', 'xc', 'cw'], ['xc'])
            TT(xm, xb, msk[:, 2, :], ALU.mult, ['B0', 'msk'], ['B1'])
            STT(xc[:, 0:T - 1], xm[:, 1:T], cw[:, 3, n:n + 1], xc[:, 0:T - 1], ALU.mult, ALU.add, ['B1', 'xc', 'cw'], ['xc'])
            ACT(xcb, xc, AF.Copy, ['xc'], ['xcb'])
            for d in range(2):
                for g in range(2):
                    dst, dk_ = (r_, 'B3') if g == 0 else (i_, 'i_')
                    for tq in range(4):
                        b = pctr[0] % 4; pctr[0] += 1
                        MM(PS[b], gw[:, (d * 2 + g) * 4 + n, :], xcb[:, tq * 512:(tq + 1) * 512], True, True,
                           ['gw', 'xcb'], [('ps', b)])
                        ACT(dst[:, tq * 512:(tq + 1) * 512], PS[b], AF.Sigmoid, [('ps', b), 'gb'], [dk_],
                            bias=gb[:, d, g, n:n + 1])
                ACT(a_, r_, AF.Exp, ['B3', 'nsp'], ['B0'], scale=nsp[:, d, n:n + 1])
                TT(u_, a_, a_, ALU.mult, ['B0'], ['B1'])
                ACT(u_, u_, AF.Sqrt, ['B1'], ['B1'], bias=1.0, scale=-1.0)
                TT(u_, u_, i_, ALU.mult, ['B1', 'i_'], ['B1'])
                TT(u_, u_, xc, ALU.mult, ['B1', 'xc'], ['B1'])
                TT(a_, a_, msk[:, 3 + d, :], ALU.mult, ['B0', 'msk'], ['B0'])
                if d == 0:
                    S.op('dve', lambda e, n=n: e.tensor_tensor_scan(out=hs0, data0=a_, data1=u_, initial=h0[:, 0, n:n + 1],
                                                                   op0=ALU.mult, op1=ALU.add), r=['B0', 'B1', 'h0'], w=['hs0'])
                    CPV(ho[:, 0, :], hs0[:, 255:T:256], ['hs0'], [('ho', 0)])
                else:
                    S.op('dve', lambda e, n=n: e.tensor_tensor_scan(out=hs1[:, ::-1], data0=a_[:, ::-1], data1=u_[:, ::-1],
                                                                   initial=h0[:, 1, n:n + 1], op0=ALU.mult, op1=ALU.add),
                         r=['B0', 'B1', 'h0'], w=['hs1'])
                    CPV(ho[:, 1, :], hs1[:, 0:T:256], ['hs1'], [('ho', 1)])
            for d in range(2):
                DMA('sp', O_LRU[:, l, d, n * 128:(n + 1) * 128].rearrange("s p -> p s"), ho[:, d, :], [('ho', d)], [], ('ho', d),
                    is_out=True, slow=True)
            TT(hs0, hs0, hs1, ALU.add, ['hs0', 'hs1'], ['hs0'])
            TT(g3, gg, gg, ALU.mult, ['gg'], ['B3'])
            TS(g3, g3, 0.044715, ALU.mult, ['B3'], ['B3'], s2=1.0, op1=ALU.add)
            TT(g3, g3, gg, ALU.mult, ['B3', 'gg'], ['B3'])
            ACT(g3, g3, AF.Sigmoid, ['B3'], ['B3'], scale=1.5957691216057308)
            TT(g3, g3, gg, ALU.mult, ['B3', 'gg'], ['B3'])
            TT(yb, hs0, g3, ALU.mult, ['hs0', 'B3'], ['yb'])
            DMA('sp', MIXT[512 + n * 128:512 + (n + 1) * 128, :], yb, ['yb'], [('MIXT', 4 + n)], 'yb')
        S.barrier()
        A.release(m)

    def mixer_ret(l):
        m = A.mark()
        W = W_IN[l]
        nbc = A.alloc([512], F32)
        DMA('sp', nbc, RET_NBC[:, l, :], [], ['nbc'], 'nbc')
        dl = A.alloc([8], F32)
        DMA('sp', dl, RET_DLBC[:, l * 8:(l + 1) * 8], [], ['dl'], 'dl')
        ACT(dl, dl, AF.Exp, ['dl'], ['dl'], scale=-1.0)
        ACT(dl, dl, AF.Ln, ['dl'], ['dl'], bias=1.0)
        TS(dl, dl, -1.0, ALU.mult, ['dl'], ['dl'])
        pos = cst[:, 13, :]
        ws = scan_ws()
        OB = [dict(qT=A.alloc([T], BF16), kT=A.alloc([T], BF16), KE=[A.alloc([NT, 128], BF16) for _ in range(2)],
                   V=A.alloc([NT, 128], BF16), G=A.alloc([NT, 128], BF16), pm=[A.alloc([128], F32) for _ in range(2)],
                   cols=A.alloc([2, 8], F32)) for _ in range(2)]
        Oacc = A.alloc([NT, 128], F32)
        ybuf = A.alloc([NT, 128], BF16); yT = A.alloc([T], BF16)
        stg = [A.alloc([128], F32) for _ in range(4)]
        sgc = [0]

        def prep_gen(h):
            o = OB[h % 2]; x = h % 2
            cols, pm = o['cols'], o['pm']
            if l == 0:
                mod_groups(5)
            for d in range(2):
                lg = dl[:, d * 4 + h:d * 4 + h + 1]
                pxi = pos[:, 0:1] if d == 0 else pos[:, 1:2]
                pze = pos[:, 2:3] if d == 0 else pos[:, 3:4]
                ACT(cols[:, d, 0:1], pxi, AF.Exp, ['cst', 'dl'], [('cols', x)], scale=lg)
                ACT(cols[:, d, 1:2], pze, AF.Exp, ['cst', 'dl'], [('cols', x)], scale=lg)
                TS(cols[:, d, 1:2], cols[:, d, 1:2], 128.0 ** -0.5, ALU.mult, [('cols', x)], [('cols', x)])
                ACT(cols[:, d, 2:3], lg, AF.Exp, ['dl'], [('cols', x)], scale=64.0)
                RECIP(cols[:, d, 3:4], cols[:, d, 0:1], [('cols', x)], [('cols', x)])
                TS(pm[d], cst[:, 2 + d, :], cols[:, d, 3:4], ALU.mult, ['cst', ('cols', x)], [('pm', x)])
            yield

            def ev_q(tq, ps, pk):
                CP(o['qT'][:, tq * 512:(tq + 1) * 512], ps, [pk], [('qT', x)])
            yield from proj_fm_g(W, OFF['c_q'] + h * 128, 128, ev_q)

            def ev_k(tq, ps, pk):
                ACT(o['kT'][:, tq * 512:(tq + 1) * 512], ps, AF.Identity, [pk], [('kT', x)], scale=128.0 ** -0.5)
            yield from proj_fm_g(W, OFF['c_k'] + h * 128, 128, ev_k)

            def ev_ktm(tt, ps, pk):
                TS(o['KE'][0][:, tt, :], ps, cols[:, 0, 1:2], ALU.mult, [pk, ('cols', x)], [('KE', x)])
                ACT(o['KE'][1][:, tt, :], ps, AF.Identity, [pk, ('cols', x)], [('KE', x)], scale=cols[:, 1, 1:2])
            yield from proj_tm_g(W, OFF['c_k'] + h * 128, 128, ev_ktm)

            def ev_v(tt, ps, pk):
                CP(o['V'][:, tt, :], ps, [pk], [('V', x)])
            yield from proj_tm_g(W, OFF['c_v'] + h * 128, 128, ev_v)

            def ev_g(tt, ps, pk):
                ACT(o['G'][:, tt, :], ps, AF.Silu, [pk], [('G', x)])
            yield from proj_tm_g(W, OFF['c_g'] + h * 128, 128, ev_g)

        gen = prep_gen(0)
        for _ in gen:
            pass
        for h in range(4):
            o = OB[h % 2]; x = h % 2
            nxt = prep_gen(h + 1) if h < 3 else None

            def init_state(d, s32, key):
                DMA('sp', s32, S_RET0[l, d, h], [], [key], ('S0', d))

            def out_state(d, seg, s32, key):
                i = sgc[0] % 4; sgc[0] += 1
                ACT(stg[i], s32, AF.Copy, [key], [('stg', i)])
                DMA('sp', O_RET[seg, l, d, h], stg[i], [('stg', i)], [], ('stg', i), is_out=True)

            cols = o['cols']
            scan_core(ws, 128, 0, [o['kT'], o['kT']], [o['qT'], o['qT']], o['V'], 128, o['KE'], o['pm'], None,
                      [cols[:, 0, 0:1], cols[:, 1, 0:1]], None, lambda d, c: cols[:, d, 2:3], init_state, out_state, Oacc,
                      [('qT', x), ('kT', x), ('KE', x), ('V', x), ('cols', x), ('pm', x)],
                      bg=(lambda: next(nxt, None)) if (nxt is not None and OVERLAP) else None)
            if nxt is not None:
                for _ in nxt:
                    pass
            head_finish(ws, Oacc, o['G'], ('G', x), nbc[:, h * 128:(h + 1) * 128], 1024 + h * 128, ybuf, yT)
        S.barrier()
        A.release(m)

    def mixer_mlstm(l):
        m = A.mark()
        W = W_IN[l]
        nbc = A.alloc([512], F32)
        DMA('sp', nbc, ML_NBC[:, l, :], [], ['nbc'], 'nbc')
        gbb = A.alloc([16], F32)
        DMA('sp', gbb, ML_GBBC[:, l, :], [], ['gbb'], 'gbb')
        em0 = A.alloc([8], F32)
        DMA('sp', em0, S_MM0BC[:, l * 8:(l + 1) * 8], [], ['em0'], 'em0')
        ACT(em0, em0, AF.Exp, ['em0'], ['em0'])
        n0 = A.alloc([2, 4], F32)
        DMA('sp', n0, S_MN0T[:, l], [], ['n0'], 'n0')
        ift = A.alloc([NT, 16], F32)
        wt, wkey = load_w(W, OFF['d_if'], 16)
        for tt in range(NT):
            for kc in range(KC):
                MM(PS[4][:, tt * 16:(tt + 1) * 16], hT[:, kc, tt * 128:(tt + 1) * 128], wt[:, kc, 0:16], kc == 0, kc == KC - 1,
                   [wkey, ('hT', tt // 4)], [('ps', 4)])
        TT(ift, PS[4][:, 0:256].rearrange("p (a b) -> p a b", a=NT), gbb.unsqueeze(1).to_broadcast([128, NT, 16]), ALU.add,
           [('ps', 4), 'gbb'], ['ift'])
        iv = ift.rearrange("p t (d g h) -> p t d g h", d=2, g=2)
        logi = A.alloc([NT, 2, 4], F32); logf = A.alloc([NT, 2, 4], F32)
        for d in range(2):
            CPV(logi[:, :, d, :], iv[:, :, d, 0, :], ['ift'], ['logi'])
            ACT(logf[:, :, d, :], iv[:, :, d, 1, :], AF.Exp, ['ift'], ['logf'], scale=-1.0)
        ACT(logf, logf, AF.Ln, ['logf'], ['logf'], bias=1.0)
        TS(logf, logf, -1.0, ALU.mult, ['logf'], ['logf'])
        pG = PS[5].rearrange("p (a b) -> p a b", a=NT)
        lf = logf.rearrange("p t d h -> p t (d h)")
        for tt in range(NT):
            MM(pG[:, tt, 0:4], cst[:, 2, :], lf[:, tt, 0:4], True, True, ['cst', 'logf'], [('ps', 5)])
            MM(pG[:, tt, 4:8], cst[:, 3, :], lf[:, tt, 4:8], True, True, ['cst', 'logf'], [('ps', 5)])
            MM(pG[:, tt, 8:16], cst[:, 8, :], lf[:, tt, :], True, True, ['cst', 'logf'], [('ps', 5)])
            MM(pG[:, tt, 16:24], cst[:, 6, :], lf[:, tt, :], True, True, ['cst', 'logf'], [('ps', 5)])
            MM(pG[:, tt, 24:32], cst[:, 7, :], lf[:, tt, :], True, True, ['cst', 'logf'], [('ps', 5)])
        Gs = A.alloc([NT, 32], F32)
        CPV(Gs, pG, [('ps', 5)], ['Gs'])
        li = logi.rearrange("p t d h -> p t (d h)")
        e_ = A.alloc([NT, 8], F32); rsc = A.alloc([NT, 8], F32); emf = A.alloc([NT, 8], F32); kes = A.alloc([NT, 8], F32)
        Ft = A.alloc([NT, 8], F32)
        edec = A.alloc([NT, 2, 8], F32)
        TT(e_, li, Gs[:, :, 0:8], ALU.subtract, ['logi', 'Gs'], ['e_'])
        ACT(rsc, e_, AF.Exp, ['e_'], ['rsc'])
        ACT(emf, Gs[:, :, 0:8], AF.Exp, ['Gs'], ['emf'], scale=-1.0)
        TT(kes, e_, Gs[:, :, 8:16], ALU.add, ['e_', 'Gs'], ['kes'])
        ACT(kes, kes, AF.Exp, ['kes'], ['kes'])
        TS(kes, kes, 128.0 ** -0.5, ALU.mult, ['kes'], ['kes'])
        ACT(edec, Gs[:, :, 16:32].rearrange("p t (a b) -> p t a b", a=2), AF.Exp, ['Gs'], ['edec'])
        CPV(Ft, Gs[:, :, 8:16], ['Gs'], ['Ft'])
        ident32 = cst[:, 0, :]
        pT = PS[6]
        TR(pT[:, 0:128], e_.rearrange("p t k -> p (t k)"), ident32, ['e_', 'cst'], [('ps', 6)])
        TR(pT[:, 128:256], Ft.rearrange("p t k -> p (t k)"), ident32, ['Ft', 'cst'], [('ps', 6)])
        GF = A.alloc([4], F32)
        S.op('dve', lambda e: e.tensor_reduce(out=GF[:, 0:2], in_=pT[:, 0:128].rearrange("p (a b) -> p a b", a=2), axis=AX.X,
                                              op=ALU.max), r=[('ps', 6)], w=['GF'])
        CPV(GF[:, 2:4], pT[:, 128:256:64], [('ps', 6)], ['GF'])
        Xg = A.alloc([2, NT, 2], F32)
        tmask = cst[:, 18, 16:32]
        for q in range(2):
            TT(Xg[:, q], tmask.unsqueeze(2).to_broadcast([128, NT, 2]),
               GF[:, 2 * q:2 * q + 2].unsqueeze(1).to_broadcast([128, NT, 2]), ALU.mult, ['GF', 'cst'], ['Xg'])
        GM = A.alloc([2, 2, 32], F32)
        for d in range(2):
            for q in range(2):
                MM(PS[7][0:4, (d * 2 + q) * 32:(d * 2 + q + 1) * 32], cst[:, 18, d * 4:d * 4 + 4],
                   Xg[:, q].rearrange("p t f -> p (t f)"), True, True, ['Xg', 'cst'], [('ps', 7)])
        CPV(GM[0:4].rearrange("p a b c -> p (a b c)"), PS[7][0:4, 0:128], [('ps', 7)], ['GM'])
        mf = A.alloc([2, 8], F32); mt = A.alloc([8], F32)
        for d in range(2):
            Gv = GM[0:4, d, 0, :].rearrange("p (s k) -> p s k", k=4)
            Fv = GM[0:4, d, 1, :].rearrange("p (s k) -> p s k", k=4)
            order = [0, 1, 2, 3] if d == 0 else [3, 2, 1, 0]
            for i, k in enumerate(order):
                if i == 0:
                    TS(mt[0:4], Gv[:, :, k], 0.0, ALU.max, ['GM'], ['mt'])
                else:
                    TT(mt[0:4], mf[0:4, d, :], Gv[:, :, k], ALU.max, ['mf', 'GM'], ['mt'])
                TT(mf[0:4, d, :], mt[0:4], Fv[:, :, k], ALU.add, ['mt', 'GM'], ['mf'])
            DMA('sp', O_MM[:, l, d, :].rearrange("s h -> h s"), mf[0:4, d, :], ['mf'], [], ('omm', d), is_out=True, slow=True)
        emfs = A.alloc([2, 8], F32)
        ACT(emfs[0:4], mf[0:4], AF.Exp, ['mf'], ['emfs'], scale=-1.0)
        ebc = A.alloc([2, 4, 8], F32)
        for d in range(2):
            for h in range(4):
                MM(PS[7][:, 256 + (d * 4 + h) * 8:256 + (d * 4 + h + 1) * 8], cst[0:4, 14 + h, :], emfs[0:4, d, :], True, True,
                   ['emfs', 'cst'], [('ps', 7)])
        CPV(ebc.rearrange("p a b c -> p (a b c)"), PS[7][:, 256:320], [('ps', 7)], ['ebc'])
        S.barrier()
        ws = scan_ws()
        OB = [dict(qT=A.alloc([T], BF16), kT=A.alloc([T], BF16), KE=[A.alloc([NT, 128], BF16) for _ in range(2)],
                   V=A.alloc([NT, 136], BF16), G=A.alloc([NT, 128], BF16)) for _ in range(2)]
        Oacc = A.alloc([NT, 128], F32)
        ybuf = A.alloc([NT, 128], BF16); yT = A.alloc([T], BF16)
        stg = [A.alloc([136], F32) for _ in range(4)]
        sgc = [0]
        for x in range(2):
            MEMSET(OB[x]['V'][:, :, 128:129], 1.0, [('V', x)])

        def prep_gen(h):
            o = OB[h % 2]; x = h % 2
            if l == 0:
                mod_groups(5)

            def ev_q(tq, ps, pk):
                CP(o['qT'][:, tq * 512:(tq + 1) * 512], ps, [pk], [('qT', x)])
            yield from proj_fm_g(W, OFF['d_q'] + h * 128, 128, ev_q)

            def ev_k(tq, ps, pk):
                ACT(o['kT'][:, tq * 512:(tq + 1) * 512], ps, AF.Identity, [pk], [('kT', x)], scale=128.0 ** -0.5)
            yield from proj_fm_g(W, OFF['d_k'] + h * 128, 128, ev_k)

            def ev_ktm(tt, ps, pk):
                TS(o['KE'][0][:, tt, :], ps, kes[:, tt, h:h + 1], ALU.mult, [pk], [('KE', x)])
                ACT(o['KE'][1][:, tt, :], ps, AF.Identity, [pk], [('KE', x)], scale=kes[:, tt, 4 + h:5 + h])
            yield from proj_tm_g(W, OFF['d_k'] + h * 128, 128, ev_ktm)

            def ev_v(tt, ps, pk):
                CP(o['V'][:, tt, 0:128], ps, [pk], [('V', x)])
            yield from proj_tm_g(W, OFF['d_v'] + h * 128, 128, ev_v)

            def ev_g(tt, ps, pk):
                ACT(o['G'][:, tt, :], ps, AF.Sigmoid, [pk], [('G', x)])
            yield from proj_tm_g(W, OFF['d_o'] + h * 128, 128, ev_g)

        for _ in prep_gen(0):
            pass
        for h in range(4):
            o = OB[h % 2]; x = h % 2
            nxt = prep_gen(h + 1) if h < 3 else None

            def init_state(d, s32, key):
                DMA('sp', s32[:, 0:128], S_MC0[l, d, h], [], [key], ('S0', d))
                CPV(s32[:, 128:129], n0[:, d, h:h + 1], [key], [key])
                TS(s32, s32, em0[:, d * 4 + h:d * 4 + h + 1], ALU.mult, [key], [key])

            def out_state(d, seg, s32, key):
                i = sgc[0] % 4; sgc[0] += 1
                ACT(stg[i][:, 0:129], s32, AF.Identity, [key], [('stg', i)], scale=ebc[:, d, h, seg:seg + 1])
                DMA('sp', O_MC[seg, l, d, h], stg[i][:, 0:128], [('stg', i)], [], ('stg', i), is_out=True)
                DMA('sp', O_MN[seg, l, d, h].rearrange("(p o) -> p o", o=1), stg[i][:, 128:129], [('stg', i)], [], ('stgn', i),
                    is_out=True)

            scan_core(ws, 128, 0, [o['kT'], o['kT']], [o['qT'], o['qT']], o['V'], 129, o['KE'], [cst[:, 2, :], cst[:, 3, :]],
                      [rsc[:, :, h], rsc[:, :, 4 + h]], None, [emf[:, :, h], emf[:, :, 4 + h]],
                      lambda d, c: edec[:, c // 2, c % 2, d * 4 + h:d * 4 + h + 1], init_state, out_state, Oacc,
                      [('qT', x), ('kT', x), ('KE', x), ('V', x)],
                      bg=(lambda: next(nxt, None)) if (nxt is not None and OVERLAP) else None)
            if nxt is not None:
                for _ in nxt:
                    pass
            head_finish(ws, Oacc, o['G'], ('G', x), nbc[:, h * 128:(h + 1) * 128], 1536 + h * 128, ybuf, yT)
        S.barrier()
        A.release(m)

    def mixer_gla(l):
        m = A.mark()
        W = W_IN[l]
        nbc = A.alloc([512], F32)
        DMA('sp', nbc, GLA_NBC[:, l, :], [], ['nbc'], 'nbc')
        upb = A.alloc([2, 256], BF16)
        DMA('pool', upb[0:64], GLA_UPB[l].rearrange("d r c -> r d c"), [], ['upb'], 'upb')
        lr1 = [A.alloc([T], BF16) for _ in range(2)]
        for d in range(2):
            MEMSET(lr1[d][0:64, :], 1.0, [('lr1', d)])

            def ev_lr(tq, ps, pk, d=d):
                CP(lr1[d][0:16, tq * 512:(tq + 1) * 512], ps, [pk], [('lr1', d)])
            proj_fm(W, OFF['a_lr'] + d * 16, 16, ev_lr)
        ws = scan_ws()
        qT = A.alloc([T], BF16); kT = A.alloc([T], BF16); ktm = A.alloc([NT, 128], BF16)
        KS = [A.alloc([T], BF16) for _ in range(2)]
        QS = [[A.alloc([T], BF16) for _ in range(2)] for _ in range(2)]
        for d in range(2):
            for hh in range(2):
                MEMSET(QS[d][hh], 0.0, ['QS'])
        MEMSET(ws['S16'], 0.0, [('S16', 0), ('S16', 1)])
        KE = [A.alloc([NT, 128], BF16) for _ in range(2)]
        V = A.alloc([NT, 256], BF16); G = A.alloc([NT, 256], BF16)
        eend = A.alloc([2, 32], F32)
        spt = [A.alloc([256], F32) for _ in range(2)]
        ebm = [A.alloc([128], F32) for _ in range(2)]
        eq = [A.alloc([128], F32) for _ in range(2)]
        Oacc = A.alloc([NT, 128], F32)
        ybuf = A.alloc([NT, 128], BF16); yT = A.alloc([T], BF16)
        stg = [A.alloc([128], F32) for _ in range(4)]
        sgc = [0]
        for hp in range(2):
            if l == 0:
                mod_groups(10)

            def ev_q(tq, ps, pk):
                ACT(qT[:, tq * 512:(tq + 1) * 512], ps, AF.Identity, [pk], ['qT'], scale=0.125)
            proj_fm(W, OFF['a_q'] + hp * 128, 128, ev_q)

            def ev_k(tq, ps, pk):
                CP(kT[:, tq * 512:(tq + 1) * 512], ps, [pk], ['kT'])
            proj_fm(W, OFF['a_k'] + hp * 128, 128, ev_k)

            def ev_ktm(tt, ps, pk):
                CP(ktm[:, tt, :], ps, [pk], ['ktm'])
            proj_tm(W, OFF['a_k'] + hp * 128, 128, ev_ktm)

            def ev_v(tt, ps, pk):
                CP(V[:, tt, :], ps, [pk], ['V'])
            proj_tm(W, OFF['a_v'] + hp * 256, 256, ev_v)

            def ev_g(tt, ps, pk):
                ACT(G[:, tt, :], ps, AF.Silu, [pk], ['G'])
            proj_tm(W, OFF['a_g'] + hp * 256, 256, ev_g)
            if getattr(build, 'gla_stop', None) == 'A':
                continue
            for tt in range(NT):
                tsl = slice(tt * 128, (tt + 1) * 128)
                for d in range(2):
                    s = d
                    b = pctr[0] % 4; pctr[0] += 1
                    MM(PS[b][:, 0:256], lr1[d][0:64, tsl], upb[0:64, d, :], True, True, [('lr1', d), 'upb'], [('ps', b)])
                    ACT(spt[s], PS[b][:, 0:256], AF.Exp, [('ps', b)], [('spt', s)], scale=-1.0)
                    ACT(spt[s], spt[s], AF.Ln, [('spt', s)], [('spt', s)], bias=1.0)
                    sph = spt[s][:, hp * 128:(hp + 1) * 128]
                    b2 = pctr[0] % 4; pctr[0] += 1
                    MM(PS[b2][:, 0:128], cst[:, 11 + d, :], sph, True, True, [('spt', s), 'cst'], [('ps', b2)])
                    ACT(ebm[s], PS[b2][:, 0:128], AF.Exp, [('ps', b2)], [('ebm', s)])
                    TT(KE[d][:, tt, :], ktm[:, tt, :], ebm[s], ALU.mult, [('ebm', s), 'ktm'], ['KE'])
                    b3 = pctr[0] % 4; pctr[0] += 1
                    MM(PS[b3][:, 0:128], sph, cst[:, 9 + d, :], True, True, [('spt', s), 'cst'], [('ps', b3)])
                    ACT(eq[s], PS[b3][:, 0:128], AF.Exp, [('ps', b3)], [('eq', s)])
                    for hh in range(2):
                        rr = slice(hh * 64, hh * 64 + 64)
                        TT(QS[d][hh][rr, tsl], qT[rr, tsl], eq[s][rr, :], ALU.mult, [('eq', s), 'qT'], ['QS'])
                    if d == 0:
                        CPV(eend[:, d, 2 * tt:2 * tt + 2], eq[s][:, 63:128:64], [('eq', s)], ['eend'])
                    else:
                        CPV(eend[:, d, 2 * tt:2 * tt + 2], eq[s][:, 0:128:64], [('eq', s)], ['eend'])
                    ACT(ebm[s], PS[b3][:, 0:128], AF.Exp, [('ps', b3)], [('ebm', s)], scale=-1.0)
                    TT(KS[d][:, tsl], kT[:, tsl], ebm[s], ALU.mult, [('ebm', s), 'kT'], ['KS'])
            if getattr(build, 'gla_stop', None) == 'B':
                continue
            for hh in range(2):
                if getattr(build, 'gla_stop', None) == 'C' and hh == 1:
                    continue
                h = hp * 2 + hh
                base = hh * 64

                def init_state(d, s32, key):
                    DMA('sp', s32, S_GLA0[l, d, h], [], [key], ('S0', d))

                def out_state(d, seg, s32, key):
                    i = sgc[0] % 4; sgc[0] += 1
                    ACT(stg[i][base:base + 64, :], s32, AF.Copy, [key], [('stg', i)])
                    DMA('sp', O_GLA[seg, l, d, h], stg[i][base:base + 64, :], [('stg', i)], [], ('stg', i), is_out=True)

                scan_core(ws, 64, base, KS, [QS[0][hh], QS[1][hh]], V[:, :, hh * 128:(hh + 1) * 128], 128,
                          [KE[0][:, :, hh * 64:(hh + 1) * 64], KE[1][:, :, hh * 64:(hh + 1) * 64]],
                          [cst[:, 2, :], cst[:, 3, :]], None, None, None,
                          lambda d, c: eend[base:base + 64, d, c:c + 1], init_state, out_state, Oacc,
                          ['QS', 'KS', 'KE', 'V', 'eend'], full_k=True, skew=0)
                head_finish(ws, Oacc, G[:, :, hh * 128:(hh + 1) * 128], 'G', nbc[:, h * 128:(h + 1) * 128], h * 128, ybuf, yT)
        S.barrier()
        A.release(m)

    def phase_wout(l, XS, XD):
        m = A.mark()
        mx = A.alloc([KC, T], BF16)
        for tq in range(4):
            DMA('sp', mx[:, :, tq * 512:(tq + 1) * 512], XTv(MIXT)[:, :, tq * 512:(tq + 1) * 512], [], [('mx', tq)], ('mx', tq))
        NX = 6
        xa = [A.alloc([512], F32) for _ in range(NX)]
        groups = [(oc, tq) for oc in range(KC) for tq in range(4)]

        def load_x(gi):
            oc, tq = groups[gi]
            s = gi % NX
            DMA('sp', xa[s], XS[oc * 128:(oc + 1) * 128, tq * 512:(tq + 1) * 512], [('XS', oc, tq)], [('xa', s)], ('xa', s))

        for gi in range(3):
            load_x(gi)
        wt = wkey = None
        for gi, (oc, tq) in enumerate(groups):
            if tq == 0:
                wt, wkey = load_w(W_OUT[l], oc * 128, 128)
            if gi + 3 < len(groups):
                load_x(gi + 3)
            b = gi % 8
            s = gi % NX
            for kc in range(KC):
                MM(PS[b], wt[:, kc, 0:128], mx[:, kc, tq * 512:(tq + 1) * 512], kc == 0, kc == KC - 1, [wkey, ('mx', tq)], [('ps', b)])
            STT(xa[s], PS[b], AB[:, l, 2, oc:oc + 1], xa[s], ALU.mult, ALU.add, [('ps', b), ('xa', s), 'AB'], [('xa', s)])
            DMA('sp', XD[oc * 128:(oc + 1) * 128, tq * 512:(tq + 1) * 512], xa[s], [('xa', s)], [('XD', oc, tq)], ('xas', s))
        S.barrier()
        A.release(m)

    def phase_ffn_up(l):
        m = A.mark()
        msk = A.alloc([2, T], BF16)
        DMA('pool', msk, MASKS[5:7, :].partition_broadcast(128), [], ['msk'], 'msk')
        cw = A.alloc([9, FC], F32); cb = A.alloc([FC], F32)
        DMA('sp', cw, FFN_CWT[:, l], [], ['cw'], 'cw')
        DMA('sp', cb, FFN_CBT[:, l], [], ['cb'], 'cb')
        u_ = [A.alloc([T], F32) for _ in range(3)]
        g_ = [A.alloc([T], F32) for _ in range(2)]
        acc = [A.alloc([T], F32) for _ in range(2)]
        gl = A.alloc([T], F32); gr = A.alloc([T], F32)
        ab = [A.alloc([T], BF16) for _ in range(2)]

        def stage_a(cc):
            us, gs = cc % 3, cc % 2
            i = wctr[0] % len(wslots); wctr[0] += 1
            wt = wslots[i]; wkey = ('w', i)
            DMA('pool', wt[:, :, 0:128], W_UP[l][:, cc * 128:(cc + 1) * 128].rearrange("(kc p) n -> p kc n", p=128), [], [wkey], ('w', i))
            DMA('pool', wt[:, :, 128:256], W_UP[l][:, DFF + cc * 128:DFF + (cc + 1) * 128].rearrange("(kc p) n -> p kc n", p=128),
                [], [wkey], ('w', i))

            def ev_u(tq, ps, pk):
                ACT(u_[us][:, tq * 512:(tq + 1) * 512], ps, AF.Copy, [pk], [('u', us)])
            proj_fm(None, 0, 128, ev_u, sub=0, wt=wt, wkey=wkey)

            def ev_g(tq, ps, pk):
                ACT(g_[gs][:, tq * 512:(tq + 1) * 512], ps, AF.Copy, [pk], [('g', gs)])
            proj_fm(None, 0, 128, ev_g, sub=128, wt=wt, wkey=wkey)

        def stage_b(cc):
            gs, s = cc % 2, cc % 2
            TT(gl, g_[gs], msk[:, 0, :], ALU.mult, [('g', gs), 'msk'], ['gl'], eng='pool')
            TT(gr, g_[gs], msk[:, 1, :], ALU.mult, [('g', gs), 'msk'], ['gr'], eng='pool')
            ACT(acc[s], g_[gs], AF.Identity, [('g', gs), 'cw', 'cb'], [('acc', s)], bias=cb[:, cc:cc + 1], scale=cw[:, 4, cc:cc + 1])
            for dc in (0, -1, 1):
                for dr in (-1, 0, 1):
                    if dr == 0 and dc == 0:
                        continue
                    off = 64 * dr + dc
                    lo, hi = max(0, -off), min(T, T - off)
                    src = gl if dc == -1 else (gr if dc == 1 else g_[gs])
                    sk = 'gl' if dc == -1 else ('gr' if dc == 1 else ('g', gs))
                    STT(acc[s][:, lo:hi], src[:, lo + off:hi + off], cw[:, (dr + 1) * 3 + dc + 1, cc:cc + 1], acc[s][:, lo:hi],
                        ALU.mult, ALU.add, [sk, ('acc', s), 'cw'], [('acc', s)])

        def stage_c(cc):
            us, s = cc % 3, cc % 2
            ACT(acc[s], acc[s], AF.Silu, [('acc', s)], [('acc', s)])
            TT(ab[s], acc[s], u_[us], ALU.mult, [('acc', s), ('u', us)], [('ab', s)])
            DMA('sp', ACTS[cc * 128:(cc + 1) * 128, :], ab[s], [('ab', s)], [('ACTS', cc)], ('ab', s))

        for it in range(FC + 2):
            if it < FC:
                stage_a(it)
            if 0 <= it - 1 < FC:
                stage_b(it - 1)
            if 0 <= it - 2 < FC:
                stage_c(it - 2)
        S.barrier()
        A.release(m)

    def phase_ffn_down(l, XS, XD):
        m = A.mark()
        at = A.alloc([FC, 1024], BF16)
        HA = Arena(hT_raw, 16384)
        wd = [HA.alloc([FC, 128], BF16) for _ in range(3)]
        xa = [HA.alloc([512], F32) for _ in range(4)]
        av = ACTS.rearrange("(cc p) t -> p cc t", p=128)
        xi = 0
        wi = 0
        for t2 in range(2):
            for ch in range(4):
                for hf in range(2):
                    DMA('sp', at[:, ch * 11:(ch + 1) * 11, hf * 512:(hf + 1) * 512],
                        av[:, ch * 11:(ch + 1) * 11, t2 * 1024 + hf * 512:t2 * 1024 + (hf + 1) * 512], [],
                        [('at', hf, ch)], ('at', hf, ch))
            for oc in range(KC):
                ws = wi % 3; wi += 1
                DMA('pool', wd[ws], W_DOWN[l][:, oc * 128:(oc + 1) * 128].rearrange("(cc p) n -> p cc n", p=128), [], [('wd', ws)],
                    ('wd', ws))
                for hf in range(2):
                    tq = t2 * 2 + hf
                    b = pctr[0] % 8; pctr[0] += 1
                    xs = xi % 4; xi += 1
                    DMA('sp', xa[xs], XS[oc * 128:(oc + 1) * 128, tq * 512:(tq + 1) * 512], [('XS', oc, tq)], [('xa', xs)], ('xa', xs))
                    for cc in range(FC):
                        MM(PS[b], wd[ws][:, cc, :], at[:, cc, hf * 512:(hf + 1) * 512], cc == 0, cc == FC - 1,
                           [('wd', ws), ('at', hf, cc // 11)], [('ps', b)])
                    STT(xa[xs], PS[b], AB[:, l, 5, oc:oc + 1], xa[xs], ALU.mult, ALU.add, [('ps', b), ('xa', xs), 'AB'], [('xa', xs)])
                    DMA('sp', XD[oc * 128:(oc + 1) * 128, tq * 512:(tq + 1) * 512], xa[xs], [('xa', xs)], [('XD', oc, tq)], ('xas', xs))
        S.barrier()
        A.release(m)

    def dump(name, src):
        if name in DUMP:
            S.barrier()
            DMA('sp', DUMP[name], src, [], [], 'dump_' + name, is_out=True)
            S.barrier()

    def program():
        phase_prologue()
        if stop_after == 'prologue':
            return
        for l in range(DEPTH):
            phase_norm(XTA, AB[:, l, 0, :], AB[:, l, 1, :])
            if l == 0 and 'hT' in DUMP:
                S.barrier()
                DMA('sp', XTv(DUMP['hT']), hT, [], [], 'dump_hT', is_out=True)
                S.barrier()
            if stop_after == 'norm1':
                return
            m = A.mark()
            wslots.clear()
            wslots.extend(A.alloc([KC, 256], BF16) for _ in range(3))
            for mx in mixers:
                {'lru': mixer_lru, 'ret': mixer_ret, 'mlstm': mixer_mlstm, 'gla': mixer_gla}[mx](l)
            if l == 0:
                mod_groups(NMODG)
                mod_finalize(0, [2, 3, 4, 5])
                mod_finalize(1, [0, 1, 2, 3, 4, 5])
                S.barrier()
            A.release(m)
            if l == 0:
                dump('mixT', MIXT)
            if stop_after == 'mix':
                return
            m = A.mark()
            wslots.clear()
            wslots.extend(A.alloc([KC, 256], BF16) for _ in range(3))
            phase_wout(l, XTA, XTB)
            A.release(m)
            if l == 0:
                dump('xtb', XTB)
            if stop_after == 'wout':
                return
            phase_norm(XTB, AB[:, l, 3, :], AB[:, l, 4, :])
            m = A.mark()
            wslots.clear()
            wslots.extend(A.alloc([KC, 256], BF16) for _ in range(3))
            phase_ffn_up(l)
            A.release(m)
            m = A.mark()
            phase_ffn_down(l, XTB, XTA)
            A.release(m)
            if l == 0:
                dump('xta', XTA)
            if stop_after == 'ffn':
                return
        phase_norm(XTA, fnT, None, final=True)

    mixers = build.mixers
    OVERLAP = getattr(build, 'overlap', True)
    program()
    S.finish()
    S.emit(nc, st)
    st.close()
    build.stats = dict(n_ops=S.n, arena_hi=A.hi, per_eng={e: len(S.q[e]) for e in ENGS})
    return nc


build.mixers = ['gla', 'lru', 'ret', 'mlstm']


def make_in_maps(inp):
    f32 = np.float32
    g = lambda k: np.asarray(inp[k], f32)
    consts = make_consts()
    shared = {
        "consts": consts,
        "w_mod": g('w_mod'), "w_in": g('w_in'), "w_out": g('w_out'), "w_up": g('w_up'), "w_down": g('w_down'),
        "b_modT": np.ascontiguousarray(fm(g('b_mod'), 96)),
        "norm1T": fm(g('norm1_w'), KC), "norm2T": fm(g('norm2_w'), KC), "fnormT": fm(g('final_norm_w'), KC),
        "gla_nbc": bc(g('gla_norm_w')).reshape(128, DEPTH, 512),
        "ret_nbc": bc(g('ret_norm_w')).reshape(128, DEPTH, 512),
        "ml_nbc": bc(g('mlstm_norm_w')).reshape(128, DEPTH, 512),
        "lru_cwT": fm(g('lru_conv_w'), 4),
        "lru_cbT": fm(g('lru_conv_b'), 4),
        "lru_gw": g('lru_gate_w'),
        "lru_gbT": fm(g('lru_gate_b'), 4),
        "lru_lamT": fm(g('lru_lambda'), 4),
        "ret_dlbc": bc(g('ret_decay_logit')),
        "ml_gbbc": bc(g('mlstm_gate_b')).reshape(128, DEPTH, 16),
        "ffn_cbT": fm(g('ffn_conv_b'), FC),
    }
    upb = np.zeros((DEPTH, 2, 64, 256), f32)
    upb[:, :, 0:16, :] = g('gla_gate_up')
    upb[:, :, 16, :] = g('gla_gate_b')
    shared["gla_upb"] = upb
    cw = g('ffn_conv_w')
    cw_s = fm(cw.reshape(DEPTH, 9, DFF), FC)
    cwp = np.zeros_like(cw)
    cwp[:, 1] = cw[:, 1]
    cw_p = fm(cwp.reshape(DEPTH, 9, DFF), FC)
    maps = []
    for core in range(8):
        d = dict(shared)
        if core < 4:
            d["x"] = np.ascontiguousarray(g('x_prompt')[core * 8:(core + 1) * 8].reshape(T, D))
            d["condT"] = fm(g('c_ctx'), KC)
            d["keep"] = np.zeros((128, 1), f32)
            d["masks"] = core_masks(256, 256, 0.0)
            d["s_gla0"] = np.zeros((DEPTH, 2, 4, 64, 128), f32)
            d["s_lru0T"] = np.zeros((128, DEPTH, 2, 4), f32)
            d["s_ret0"] = np.zeros((DEPTH, 2, 4, 128, 128), f32)
            d["s_mc0"] = np.zeros((DEPTH, 2, 4, 128, 128), f32)
            d["s_mn0T"] = np.zeros((128, DEPTH, 2, 4), f32)
            d["s_mm0bc"] = np.zeros((128, DEPTH * 8), f32)
            d["ffn_cwT"] = cw_p
        else:
            b = core - 4
            d["x"] = np.ascontiguousarray(g('x_sample')[b])
            d["condT"] = fm(g('c')[b], KC)
            d["keep"] = np.ones((128, 1), f32)
            d["masks"] = core_masks(T, 64, 1.0)
            d["s_gla0"] = np.ascontiguousarray(g('state_gla')[b])
            d["s_lru0T"] = fm(g('state_lru')[b], 4)
            d["s_ret0"] = np.ascontiguousarray(g('state_ret')[b])
            d["s_mc0"] = np.ascontiguousarray(g('state_mlstm_c')[b])
            d["s_mn0T"] = np.ascontiguousarray(np.moveaxis(g('state_mlstm_n')[b], -1, 0))
            d["s_mm0bc"] = bc(g('state_mlstm_m')[b])
            d["ffn_cwT"] = cw_s
        maps.append(d)
    return maps


_NC = {}


def kernel(**inputs):
    if 'nc' not in _NC:
        _NC['nc'] = build()
    nc = _NC['nc']
    maps = make_in_maps(inputs)
    res = run_bass_kernel_spmd(nc, maps, core_ids=list(range(8)))
    R = res.results
    y_prompt = np.concatenate([R[i]["y"].reshape(8, 256, D) for i in range(4)], axis=0)
    y_sample = np.stack([R[4 + i]["y"] for i in range(4)], axis=0)

    def cat(name):
        return np.concatenate([R[i][name] for i in range(4)], axis=0)
    return (y_prompt.astype(np.float32), y_sample.astype(np.float32), cat("o_gla"), cat("o_lru"), cat("o_ret"),
            cat("o_mc"), cat("o_mn"), cat("o_mm"))
```

```python
import numpy as np
from contextlib import ExitStack
import concourse.bass as bass
import concourse.mybir as mybir
from concourse.bass_utils import run_bass_kernel_spmd

F32 = mybir.dt.float32
BF16 = mybir.dt.bfloat16
AF = mybir.ActivationFunctionType
ALU = mybir.AluOpType
AX = mybir.AxisListType
ENGS = ['pe', 'act', 'dve', 'pool', 'sp']
EPOCH = 12000

T = 2048
D = 2048
KC = 16
NT = 16
DFF = 5632
FC = 44
INW = 6704
DEPTH = 2
EPS = 1e-6
OFF = dict(a_q=0, a_k=256, a_v=512, a_g=1024, a_lr=1536, b_x=1568, b_g=2080, c_q=2592, c_k=3104, c_v=3616,
           c_g=4128, d_q=4640, d_k=5152, d_v=5664, d_o=6176, d_if=6688)
NCONST = 20


class Op:
    __slots__ = ('eng', 'fn', 'deps', 'tok', 'sig', 'dma_key', 'val', 'idx')


class Sched:
    def __init__(self):
        self.q = {e: [] for e in ENGS}
        self.lastw = {}
        self.readers = {}
        self.dcount = {}
        self.dma_since = []
        self.out_dmas = []
        self.n = 0

    def _new(self, eng, fn, dma_key):
        o = Op()
        o.eng = eng; o.fn = fn; o.deps = {}; o.tok = None; o.sig = False
        o.dma_key = dma_key; o.val = 0; o.idx = self.n
        self.n += 1
        return o

    def op(self, eng, fn, r=(), w=(), dma_key=None, is_out=False):
        o = self._new(eng, fn, dma_key)
        for k in r:
            lw = self.lastw.get(k)
            if lw is not None:
                o.deps[lw] = 'raw'
        for k in w:
            lw = self.lastw.get(k)
            if lw is not None and lw not in o.deps:
                o.deps[lw] = 'waw'
            for rd in self.readers.get(k, ()):
                if rd not in o.deps:
                    o.deps[rd] = 'war'
        for k in r:
            lst = self.readers.setdefault(k, [])
            if dma_key is None:
                lst[:] = [x for x in lst if not (x.dma_key is None and x.eng == eng)]
            lst.append(o)
        for k in w:
            self.lastw[k] = o
            self.readers[k] = []
        o.deps.pop(o, None)
        if dma_key is not None:
            self.dcount[dma_key] = self.dcount.get(dma_key, 0) + 16
            o.val = self.dcount[dma_key]
            self.dma_since.append(o)
            if is_out:
                self.out_dmas.append(o)
        self.q[eng].append(o)
        return o

    def barrier(self):
        deps = {}
        for e in ENGS:
            for o in reversed(self.q[e]):
                if o.fn is not None and o.dma_key is None:
                    deps[o] = 'raw'
                    break
        for o in self.dma_since:
            deps[o] = 'raw'
        for e in ENGS:
            b = self._new(e, None, None)
            b.deps = dict(deps)
            self.q[e].append(b)
        self.dma_since = []
        self.lastw = {}
        self.readers = {}

    def finish(self):
        b = self._new('sp', None, None)
        b.deps = {o: 'raw' for o in self.out_dmas}
        self.q['sp'].append(b)

    def keep(self, o, dep, kind):
        if dep.dma_key is not None or o.dma_key is not None:
            return True
        if dep.eng != o.eng:
            return True
        if o.eng == 'pe':
            return False
        return kind == 'raw'

    def emit(self, nc, stack):
        for e in ENGS:
            for o in self.q[e]:
                for d, kind in o.deps.items():
                    if self.keep(o, d, kind):
                        d.sig = True
        esems = {}
        for e in ENGS:
            ns = sum(1 for o in self.q[e] if o.sig and o.dma_key is None)
            esems[e] = [stack.enter_context(nc.semaphore(f"s_{e}_{i}")) for i in range(ns // EPOCH + 1)]
        dsems = {}
        for i, k in enumerate(self.dcount):
            dsems[k] = stack.enter_context(nc.semaphore(f"d_{i}"))
        for e in ENGS:
            c = 0
            for o in self.q[e]:
                if o.dma_key is not None:
                    o.tok = (dsems[o.dma_key], o.val)
                elif o.sig:
                    o.tok = (esems[e][c // EPOCH], c % EPOCH + 1)
                    c += 1
        sched = self

        def replay(e, eng):
            waited = {}
            for o in sched.q[e]:
                for d in sorted(o.deps, key=lambda x: x.idx):
                    if not sched.keep(o, d, o.deps[d]):
                        continue
                    sem, val = d.tok
                    key = id(sem)
                    if waited.get(key, 0) < val:
                        eng.wait_ge(sem, val)
                        waited[key] = val
                if o.fn is None:
                    continue
                ins = o.fn(eng)
                if o.tok is not None:
                    ins.then_inc(o.tok[0], 16 if o.dma_key is not None else 1)

        with nc.Block() as block:
            @block.tensor
            def _(eng):
                replay('pe', eng)

            @block.scalar
            def _(eng):
                replay('act', eng)

            @block.vector
            def _(eng):
                replay('dve', eng)

            @block.gpsimd
            def _(eng):
                replay('pool', eng)

            @block.sync
            def _(eng):
                replay('sp', eng)


class Arena:
    def __init__(self, ap, nwords):
        self.ap = ap; self.n = nwords; self.off = 0; self.hi = 0

    def alloc(self, free_shape, dtype):
        nel = int(np.prod(free_shape))
        nw = nel if dtype == F32 else (nel + 1) // 2
        nw = (nw + 7) // 8 * 8
        assert self.off + nw <= self.n, f"arena overflow {self.off}+{nw}>{self.n}"
        a = self.ap[:, self.off:self.off + nw]
        self.off += nw; self.hi = max(self.hi, self.off)
        if dtype != F32:
            a = a.bitcast(dtype)
        a = a[:, 0:nel]
        if len(free_shape) == 2:
            a = a.rearrange("p (a b) -> p a b", a=free_shape[0])
        elif len(free_shape) == 3:
            a = a.rearrange("p (a b c) -> p a b c", a=free_shape[0], b=free_shape[1])
        return a

    def mark(self):
        return self.off

    def release(self, m):
        self.off = m


def make_consts():
    p = np.arange(128)
    blk = (p[:, None] // 64) == (p[None, :] // 64)
    c = np.zeros((128, NCONST, 128), np.float32)
    c[:, 0, :] = np.eye(128)
    c[:, 1, :] = 1.0
    c[:, 2, :] = blk & (p[:, None] <= p[None, :])
    c[:, 3, :] = blk & (p[:, None] >= p[None, :])
    c[:, 4, :] = blk & (p[:, None] > p[None, :])
    c[:, 5, :] = blk & (p[:, None] < p[None, :])
    c[:, 6, :] = (p[:, None] < 64)
    c[:, 7, :] = (p[:, None] >= 64)
    c[:, 8, :] = blk
    for i in range(4):
        c[:, 9 + i, :] = c[:, 2 + i, :] * (-1.0 / 16.0)
    c[:, 13, 0] = p % 64 + 1
    c[:, 13, 1] = 64 - p % 64
    c[:, 13, 2] = 63 - p % 64
    c[:, 13, 3] = p % 64
    for h in range(4):
        c[h, 14 + h, :] = 1.0
    c[:, 18, 0:4] = (p[:, None] % 8) == np.arange(4)[None, :]
    c[:, 18, 4:8] = (p[:, None] % 8) == (4 + np.arange(4))[None, :]
    c[:, 18, 16:32] = (p[:, None] // 8) == np.arange(16)[None, :]
    return c


def fm(v, n):
    v = np.asarray(v, np.float32)
    lead = v.shape[:-1]
    return np.ascontiguousarray(np.moveaxis(v.reshape(lead + (n, 128)), -1, 0))


def bc(v):
    v = np.asarray(v, np.float32).reshape(-1)
    return np.ascontiguousarray(np.broadcast_to(v[None, :], (128, v.size)))


def core_masks(seq_len, grid_w, keep):
    t = np.arange(T)
    pos = t % seq_len
    m = np.ones((7, T), np.float32)
    m[0] = pos <= seq_len - 3
    m[1] = pos <= seq_len - 2
    m[2] = pos >= 1
    m[3] = np.where((pos == 0) & (t > 0), keep, 1.0)
    m[4] = np.where((pos == seq_len - 1) & (t < T - 1), keep, 1.0)
    col = t % grid_w
    m[5] = col != grid_w - 1
    m[6] = col != 0
    return m


def build(stop_after=None, dumps=()):
    nc = bass.Bass("TRN2", target_bir_lowering=False)

    def din(name, shape, dt=F32):
        return nc.dram_tensor(name, list(shape), dt, kind="ExternalInput").ap()

    def dout(name, shape, dt=F32):
        return nc.dram_tensor(name, list(shape), dt, kind="ExternalOutput").ap()

    def dscr(name, shape, dt=F32):
        return nc.dram_tensor(name, list(shape), dt, kind="Internal").ap()

    X = din("x", [T, D])
    CONDT = din("condT", [128, KC])
    KEEP = din("keep", [128, 1])
    MASKS = din("masks", [7, T])
    CONSTS = din("consts", [128, NCONST, 128])
    W_MOD = din("w_mod", [DEPTH, D, 6 * D])
    B_MODT = din("b_modT", [128, DEPTH, 96])
    NORM1T = din("norm1T", [128, DEPTH, KC])
    NORM2T = din("norm2T", [128, DEPTH, KC])
    FNORMT = din("fnormT", [128, KC])
    W_IN = din("w_in", [DEPTH, D, INW])
    W_OUT = din("w_out", [DEPTH, D, D])
    W_UP = din("w_up", [DEPTH, D, 2 * DFF])
    W_DOWN = din("w_down", [DEPTH, DFF, D])
    GLA_UPB = din("gla_upb", [DEPTH, 2, 64, 256])
    GLA_NBC = din("gla_nbc", [128, DEPTH, 512])
    S_GLA0 = din("s_gla0", [DEPTH, 2, 4, 64, 128])
    LRU_CWT = din("lru_cwT", [128, DEPTH, 4, 4])
    LRU_CBT = din("lru_cbT", [128, DEPTH, 4])
    LRU_GW = din("lru_gw", [DEPTH, 2, 2, 4, 128, 128])
    LRU_GBT = din("lru_gbT", [128, DEPTH, 2, 2, 4])
    LRU_LAMT = din("lru_lamT", [128, DEPTH, 2, 4])
    S_LRU0T = din("s_lru0T", [128, DEPTH, 2, 4])
    RET_DLBC = din("ret_dlbc", [128, DEPTH * 2 * 4])
    RET_NBC = din("ret_nbc", [128, DEPTH, 512])
    S_RET0 = din("s_ret0", [DEPTH, 2, 4, 128, 128])
    ML_GBBC = din("ml_gbbc", [128, DEPTH, 16])
    ML_NBC = din("ml_nbc", [128, DEPTH, 512])
    S_MC0 = din("s_mc0", [DEPTH, 2, 4, 128, 128])
    S_MN0T = din("s_mn0T", [128, DEPTH, 2, 4])
    S_MM0BC = din("s_mm0bc", [128, DEPTH * 2 * 4])
    FFN_CWT = din("ffn_cwT", [128, DEPTH, 9, FC])
    FFN_CBT = din("ffn_cbT", [128, DEPTH, FC])

    Y = dout("y", [T, D])
    O_GLA = dout("o_gla", [8, DEPTH, 2, 4, 64, 128])
    O_LRU = dout("o_lru", [8, DEPTH, 2, 512])
    O_RET = dout("o_ret", [8, DEPTH, 2, 4, 128, 128])
    O_MC = dout("o_mc", [8, DEPTH, 2, 4, 128, 128])
    O_MN = dout("o_mn", [8, DEPTH, 2, 4, 128])
    O_MM = dout("o_mm", [8, DEPTH, 2, 4])

    XTA = dscr("xta", [D, T])
    XTB = dscr("xtb", [D, T])
    MIXT = dscr("mixt", [D, T], BF16)
    MODROW = dscr("modrow", [DEPTH, 6 * D])
    ACTS = dscr("acts", [DFF, T], BF16)
    DUMP = {}
    if 'hT' in dumps:
        DUMP['hT'] = dout("dump_hT", [D, T], BF16)
    if 'mixT' in dumps:
        DUMP['mixT'] = dout("dump_mixT", [D, T], BF16)
    if 'xtb' in dumps:
        DUMP['xtb'] = dout("dump_xtb", [D, T])
    if 'xta' in dumps:
        DUMP['xta'] = dout("dump_xta", [D, T])
    if 'mod' in dumps:
        DUMP['mod'] = dout("dump_mod", [128, 192])

    S = Sched()
    st = ExitStack()
    NW = 53000
    arena_t = st.enter_context(nc.sbuf_tensor("arena", [128, NW], F32))
    ps_t = st.enter_context(nc.psum_tensor("ps", [128, 8 * 512], F32))
    A = Arena(arena_t[:, :], NW)
    PSA = ps_t[:, :]
    PS = [PSA[:, b * 512:(b + 1) * 512] for b in range(8)]

    def MM(out, lhsT, rhs, start, stop, r, w):
        return S.op('pe', lambda e: e.matmul(out, lhsT=lhsT, rhs=rhs, start=start, stop=stop), r=r, w=w)

    def TR(out, in_, ident, r, w):
        return S.op('pe', lambda e: e.transpose(out, in_, ident), r=r, w=w)

    def ACT(out, in_, func, r, w, bias=None, scale=None, accum=None):
        kw = {}
        if bias is not None:
            kw['bias'] = bias
        if scale is not None:
            kw['scale'] = scale
        if accum is not None:
            kw['accum_out'] = accum
        return S.op('act', lambda e: e.activation(out=out, in_=in_, func=func, **kw), r=r, w=w)

    def TT(out, a, b, op, r, w, eng='dve'):
        return S.op(eng, lambda e: e.tensor_tensor(out=out, in0=a, in1=b, op=op), r=r, w=w)

    def TS(out, a, s1, op0, r, w, s2=None, op1=None):
        if op1 is None:
            return S.op('dve', lambda e: e.tensor_scalar(out=out, in0=a, scalar1=s1, scalar2=None, op0=op0), r=r, w=w)
        return S.op('dve', lambda e: e.tensor_scalar(out=out, in0=a, scalar1=s1, scalar2=s2, op0=op0, op1=op1), r=r, w=w)

    def STT(out, a, s, b, op0, op1, r, w):
        return S.op('dve', lambda e: e.scalar_tensor_tensor(out=out, in0=a, scalar=s, in1=b, op0=op0, op1=op1), r=r, w=w)

    def CPV(out, in_, r, w):
        return S.op('dve', lambda e: e.tensor_copy(out=out, in_=in_), r=r, w=w)

    def RECIP(out, in_, r, w):
        return S.op('dve', lambda e: e.reciprocal(out=out, in_=in_), r=r, w=w)

    def MEMSET(out, v, w, eng='dve'):
        return S.op(eng, lambda e: e.memset(out, v), w=w)

    def DMA(eng, out, in_, r, w, key, is_out=False, slow=False):
        if slow:
            return S.op(eng, lambda e: e.dma_start(out=out, in_=in_, allow_slow_non_contiguous=True), r=r, w=w, dma_key=key, is_out=is_out)
        return S.op(eng, lambda e: e.dma_start(out=out, in_=in_), r=r, w=w, dma_key=key, is_out=is_out)

    cpc = [0]

    def CP(out, in_, r, w):
        cpc[0] += 1
        if cpc[0] % 2:
            return ACT(out, in_, AF.Copy, r, w)
        return CPV(out, in_, r, w)

    cst = A.alloc([NCONST, 128], F32)
    identb = A.alloc([128], BF16)
    onesb = A.alloc([128], BF16)
    modT = A.alloc([DEPTH * 96], F32)
    AB = A.alloc([DEPTH, 6, KC], F32)
    fnT = A.alloc([KC], F32)
    keepc = A.alloc([1], F32)
    n1T = A.alloc([DEPTH, KC], F32)
    n2T = A.alloc([DEPTH, KC], F32)
    epsc = A.alloc([1], F32)
    DMA('sp', cst, CONSTS, [], ['cst'], 'cst')
    DMA('sp', fnT, FNORMT, [], ['fnT'], 'fnT')
    DMA('sp', keepc, KEEP, [], ['keepc'], 'keepc')
    DMA('sp', n1T, NORM1T, [], ['n1T'], 'n1T')
    DMA('sp', n2T, NORM2T, [], ['n2T'], 'n2T')
    CPV(identb, cst[:, 0, :], ['cst'], ['identb'])
    CPV(onesb, cst[:, 1, :], ['cst'], ['onesb'])
    MEMSET(epsc, EPS, ['epsc'])
    ident = cst[:, 0, :]
    hT_off = A.off
    hT = A.alloc([KC, T], BF16)
    hT_raw = arena_t[:, hT_off:hT_off + 16384]
    base_mark = A.mark()

    def XTv(XT):
        return XT.rearrange("(kc p) t -> p kc t", p=128)

    scb = A.alloc([KC], BF16)
    bmt = A.alloc([DEPTH * 96], F32)
    mrow = [A.alloc([256], F32) for _ in range(2)]
    mod_state = dict(next=0, mr=0)
    NMODG = DEPTH * 48

    def mod_groups(n):
        for _ in range(n):
            gi = mod_state['next']
            if gi >= NMODG:
                return
            mod_state['next'] += 1
            l, g = divmod(gi, 48)
            wt, wkey = load_w(W_MOD[l], g * 256, 256)
            b = pctr[0] % 4; pctr[0] += 1
            for kc in range(KC):
                MM(PS[b][0:1, 0:256], scb[:, kc:kc + 1], wt[:, kc, 0:256], kc == 0, kc == KC - 1, [wkey, 'scb'], [('ps', b)])
            r = mod_state['mr'] % 2; mod_state['mr'] += 1
            ACT(mrow[r][0:1, :], PS[b][0:1, 0:256], AF.Copy, [('ps', b)], [('mrow', r)])
            DMA('sp', MODROW[l:l + 1, g * 256:(g + 1) * 256], mrow[r][0:1, :], [('mrow', r)], [('MODROW', gi)], ('mrow', r))

    def mod_finalize(l, parts):
        for part in parts:
            c0 = l * 96 + part * 16
            DMA('sp', modT[:, c0:c0 + 16], MODROW[l, part * 2048:(part + 1) * 2048].rearrange("(j p) -> p j", p=128),
                [('MODROW', l * 48 + part * 8 + i) for i in range(8)], [('modT', l, part)], ('modT', l, part), slow=True)
            TT(modT[:, c0:c0 + 16], modT[:, c0:c0 + 16], bmt[:, c0:c0 + 16], ALU.add, [('modT', l, part), 'bmt'], [('modT', l, part)])
        mo = modT[:, l * 96:(l + 1) * 96]
        if 1 in parts:
            STT(AB[:, l, 0, :], mo[:, 16:32], 1.0, n1T[:, l, :], ALU.add, ALU.mult, [('modT', l, 1), 'n1T'], ['AB'])
        if 0 in parts:
            CPV(AB[:, l, 1, :], mo[:, 0:16], [('modT', l, 0)], ['AB'])
        if 2 in parts:
            CPV(AB[:, l, 2, :], mo[:, 32:48], [('modT', l, 2)], ['AB'])
        if 4 in parts:
            STT(AB[:, l, 3, :], mo[:, 64:80], 1.0, n2T[:, l, :], ALU.add, ALU.mult, [('modT', l, 4), 'n2T'], ['AB'])
        if 3 in parts:
            CPV(AB[:, l, 4, :], mo[:, 48:64], [('modT', l, 3)], ['AB'])
        if 5 in parts:
            CPV(AB[:, l, 5, :], mo[:, 80:96], [('modT', l, 5)], ['AB'])

    def phase_prologue():
        m = A.mark()
        xin = [A.alloc([D], F32) for _ in range(2)]
        xo = [A.alloc([KC, 128], F32) for _ in range(2)]
        condt = A.alloc([KC], F32)
        wslots.clear()
        wslots.extend(A.alloc([KC, 256], BF16) for _ in range(3))
        DMA('sp', condt, CONDT, [], ['condt'], 'condt')
        DMA('sp', bmt, B_MODT.rearrange("p l j -> p (l j)"), [], ['bmt'], 'bmt')
        ACT(scb, condt, AF.Silu, ['condt'], ['scb'])
        xtv = XTv(XTA)
        nb = 0
        for tt in range(NT):
            s = tt % 2
            DMA('sp', xin[s], X[tt * 128:(tt + 1) * 128, :], [], [('xin', s)], ('xin', s))
            for g in range(4):
                b = 4 + nb % 4; nb += 1
                for j in range(4):
                    kc = g * 4 + j
                    TR(PS[b][:, j * 128:(j + 1) * 128], xin[s][:, kc * 128:(kc + 1) * 128], ident,
                       [('xin', s), 'cst'], [('ps', b)])
                CP(xo[s][:, g * 4:(g + 1) * 4, :], PS[b].rearrange("p (a b) -> p a b", a=4), [('ps', b)], [('xo', s)])
            DMA('sp', xtv[:, :, tt * 128:(tt + 1) * 128], xo[s], [('xo', s)], [('XTA', tt // 4)], ('xo', s))
            mod_groups(1)
        mod_finalize(0, [0, 1])
        S.barrier()
        A.release(m)

    def phase_norm(XT, acol, bcol, final=False):
        m = A.mark()
        xt = [A.alloc([KC, 512], F32) for _ in range(2)]
        sq = A.alloc([KC, 512], BF16)
        s1 = A.alloc([512], F32)
        rstd = A.alloc([512], F32)
        tmp = [A.alloc([512], F32) for _ in range(2)]
        if final:
            yT = [hT_raw[:, i * 8192:(i + 1) * 8192].rearrange("p (a b) -> p a b", a=KC) for i in range(2)]
            yo = [A.alloc([D], F32) for _ in range(2)]
        xtv = XTv(XT)
        nb = 0
        for tq in range(4):
            s = tq % 2
            DMA('sp', xt[s], xtv[:, :, tq * 512:(tq + 1) * 512], [('XT', tq)], [('xt', s)], ('xt', s))
            ACT(sq, xt[s], AF.Square, [('xt', s)], ['sq'])
            for kc in range(KC):
                MM(PS[4], onesb, sq[:, kc, :], kc == 0, kc == KC - 1, ['onesb', 'sq'], [('ps', 4)])
            ACT(s1, PS[4], AF.Sqrt, [('ps', 4), 'epsc'], ['s1'], bias=epsc[:, 0:1], scale=1.0 / D)
            RECIP(rstd, s1, ['s1'], ['rstd'])
            for kc in range(KC):
                ts_ = kc % 2
                TT(tmp[ts_], xt[s][:, kc, :], rstd, ALU.mult, [('xt', s), 'rstd'], [('tmp', ts_)])
                if final:
                    dst = yT[s][:, kc, :]
                    wkey = ('yT', s)
                else:
                    dst = hT[:, kc, tq * 512:(tq + 1) * 512]
                    wkey = ('hT', tq)
                if bcol is None:
                    ACT(dst, tmp[ts_], AF.Identity, [('tmp', ts_), 'AB', 'fnT'], [wkey], scale=acol[:, kc:kc + 1])
                else:
                    ACT(dst, tmp[ts_], AF.Identity, [('tmp', ts_), 'AB'], [wkey], bias=bcol[:, kc:kc + 1], scale=acol[:, kc:kc + 1])
            if final:
                for t4 in range(4):
                    tt = tq * 4 + t4
                    ys = tt % 2
                    for g in range(4):
                        b = nb % 4; nb += 1
                        for j in range(4):
                            kc = g * 4 + j
                            TR(PS[b][:, j * 128:(j + 1) * 128], yT[s][:, kc, t4 * 128:(t4 + 1) * 128], ident,
                               [('yT', s), 'cst'], [('ps', b)])
                        CP(yo[ys][:, g * 512:(g + 1) * 512], PS[b], [('ps', b)], [('yo', ys)])
                    DMA('sp', Y[tt * 128:(tt + 1) * 128, :], yo[ys], [('yo', ys)], [], ('yo', ys), is_out=True)
        S.barrier()
        A.release(m)

    wslots = []
    wctr = [0]
    pctr = [0]

    def load_w(W2d, col0, ncols, nk=KC):
        i = wctr[0] % len(wslots); wctr[0] += 1
        wt = wslots[i]
        DMA('pool', wt[:, 0:nk, 0:ncols], W2d[:, col0:col0 + ncols].rearrange("(kc p) n -> p kc n", p=128),
            [], [('w', i)], ('w', i))
        return wt, ('w', i)

    def proj_fm_g(W2d, col0, M, evac, sub=0, wt=None, wkey=None):
        if wt is None:
            wt, wkey = load_w(W2d, col0, M)
        for tq in range(4):
            b = pctr[0] % 4; pctr[0] += 1
            for kc in range(KC):
                MM(PS[b][0:M, :], wt[:, kc, sub:sub + M], hT[:, kc, tq * 512:(tq + 1) * 512], kc == 0, kc == KC - 1,
                   [wkey, ('hT', tq)], [('ps', b)])
                if kc % 4 == 3 and kc != KC - 1:
                    yield
            evac(tq, PS[b][0:M, :], ('ps', b))
            yield

    def proj_tm_g(W2d, col0, N, evac):
        wt, wkey = load_w(W2d, col0, N)
        for tt in range(NT):
            b = pctr[0] % 4; pctr[0] += 1
            for kc in range(KC):
                MM(PS[b][:, 0:N], hT[:, kc, tt * 128:(tt + 1) * 128], wt[:, kc, 0:N], kc == 0, kc == KC - 1,
                   [wkey, ('hT', tt // 4)], [('ps', b)])
                if kc % 8 == 7 and kc != KC - 1:
                    yield
            evac(tt, PS[b][:, 0:N], ('ps', b))
            yield

    def proj_fm(*a, **k):
        for _ in proj_fm_g(*a, **k):
            pass

    def proj_tm(*a, **k):
        for _ in proj_tm_g(*a, **k):
            pass

    def scan_ws():
        return dict(S32=A.alloc([2, 136], F32), S16=A.alloc([2, 136], BF16), PT=[A.alloc([64], BF16) for _ in range(4)],
                    dd=[A.alloc([2], F32) for _ in range(4)], sq=A.alloc([NT, 128], F32), ss=A.alloc([NT], F32),
                    rs=A.alloc([NT], F32), pi=[0])

    def scan_core(ws, dk, base, AT, QT, V, nv, KE, pmask, rowscale, oscale, emF, decay, init_state, out_state, Oacc, rkeys,
                  full_k=False, skew=0, bg=None):
        rows = slice(base, base + dk)
        krows = slice(0, 128) if full_k else rows
        S32, S16, PT, dd = ws['S32'], ws['S16'], ws['PT'], ws['dd']
        for d in range(2):
            init_state(d, S32[rows, d, 0:nv], ('S32', d))
            CP(S16[rows, d, 0:nv], S32[rows, d, 0:nv], [('S32', d)], [('S16', d)])
        steps = [(stp, d) for stp in range(32) for d in range(2)]
        NSTEP = len(steps)
        written = set()

        def geo(i):
            stp, d = steps[i]
            c = stp if d == 0 else 31 - stp
            tt, hb = c // 2, (c % 2) * 64
            return stp, d, c, tt, slice(hb, hb + 64), slice(c * 64, (c + 1) * 64)

        pi0 = ws['pi'][0]
        ws['pi'][0] += NSTEP

        def slots(i):
            pi = pi0 + i
            return pi % 4, pi % 4, pi % 2

        def psum_aps(i):
            stp, d, c, tt, hs, tok = geo(i)
            pslot, oslot, sslot = slots(i)
            pP = PS[4][hs, pslot * 64:(pslot + 1) * 64]
            pO = PS[5 + oslot // 2][hs, (oslot % 2) * 256:(oslot % 2) * 256 + nv]
            pS = PS[7][rows, sslot * 256:sslot * 256 + nv]
            return pP, pO, pS, ('psP', pslot), ('psO', oslot), ('psS', sslot)

        def emit_P(i):
            stp, d, c, tt, hs, tok = geo(i)
            pP, pO, pS, kP, kO, kS = psum_aps(i)
            MM(pP, AT[d][krows, tok], QT[d][krows, tok], True, True, rkeys, [kP])

        def emit_M(i):
            stp, d, c, tt, hs, tok = geo(i)
            pslot = slots(i)[0]
            pP, pO, pS, kP, kO, kS = psum_aps(i)
            if rowscale is None:
                TT(PT[pslot][hs, :], pP, pmask[d][hs, hs], ALU.mult, [kP] + rkeys, [('PT', pslot)])
            else:
                STT(PT[pslot][hs, :], pP, rowscale[d][hs, tt:tt + 1], pmask[d][hs, hs], ALU.mult, ALU.mult,
                    [kP] + rkeys, [('PT', pslot)])

        def emit_O(i):
            stp, d, c, tt, hs, tok = geo(i)
            pslot = slots(i)[0]
            pP, pO, pS, kP, kO, kS = psum_aps(i)
            MM(pO, PT[pslot][hs, :], V[hs, tt, 0:nv], True, False, [('PT', pslot)] + rkeys, [kO])
            MM(pO, QT[d][krows, tok], S16[krows, d, 0:nv], False, True, rkeys + [('S16', d)], [kO])

        def emit_E(i):
            stp, d, c, tt, hs, tok = geo(i)
            oslot = slots(i)[1]
            pP, pO, pS, kP, kO, kS = psum_aps(i)
            oa = Oacc[hs, tt, :]
            oak = ('O', c)
            first = c not in written
            written.add(c)
            if emF is not None:
                dsl = dd[oslot][hs, :]
                ACT(dsl[:, 0:1], pO[:, 128:129], AF.Abs, [kO], [('dd', oslot)])
                TS(dsl[:, 0:1], dsl[:, 0:1], emF[d][hs, tt:tt + 1], ALU.max, [('dd', oslot)] + rkeys, [('dd', oslot)])
                RECIP(dsl[:, 1:2], dsl[:, 0:1], [('dd', oslot)], [('dd', oslot)])
                sc = dsl[:, 1:2]
                sck = [('dd', oslot)]
            elif oscale is not None:
                sc = oscale[d][hs, 0:1]
                sck = []
            else:
                sc = None
                sck = []
            if sc is None:
                if first:
                    CPV(oa, pO[:, 0:128], [kO], [oak])
                else:
                    TT(oa, pO[:, 0:128], oa, ALU.add, [kO, oak], [oak])
            else:
                if first:
                    TS(oa, pO[:, 0:128], sc, ALU.mult, [kO] + sck + rkeys, [oak])
                else:
                    STT(oa, pO[:, 0:128], sc, oa, ALU.mult, ALU.add, [kO, oak] + sck + rkeys, [oak])

        def emit_S(i):
            stp, d, c, tt, hs, tok = geo(i)
            pP, pO, pS, kP, kO, kS = psum_aps(i)
            MM(pS, KE[d][hs, tt, 0:dk], V[hs, tt, 0:nv], True, True, rkeys, [kS])

        def emit_U(i):
            stp, d, c, tt, hs, tok = geo(i)
            pP, pO, pS, kP, kO, kS = psum_aps(i)
            STT(S32[rows, d, 0:nv], S32[rows, d, 0:nv], decay(d, c), pS, ALU.mult, ALU.add,
                [('S32', d), kS] + rkeys, [('S32', d)])
            seg_end = (c % 4 == 3) if d == 0 else (c % 4 == 0)
            if seg_end:
                out_state(d, c // 4, S32[rows, d, 0:nv], ('S32', d))
                if stp != 31:
                    TS(S32[rows, d, 0:nv], S32[rows, d, 0:nv], keepc[rows, 0:1], ALU.mult,
                       [('S32', d), 'keepc'], [('S32', d)])
            if stp != 31:
                ACT(S16[rows, d, 0:nv], S32[rows, d, 0:nv], AF.Copy, [('S32', d)], [('S16', d)])

        skew = getattr(build, 'scan_skew', skew)
        if skew == 0:
            for i in range(NSTEP):
                emit_P(i); emit_M(i)
                if bg is not None:
                    bg()
                emit_O(i); emit_E(i)
                if bg is not None:
                    bg()
                emit_S(i); emit_U(i)
                if bg is not None:
                    bg()
        elif skew == 2:
            emit_P(0)
            emit_M(0)
            emit_S(0)
            for i in range(NSTEP):
                if i + 1 < NSTEP:
                    emit_P(i + 1)
                emit_O(i)
                emit_E(i)
                emit_U(i)
                if i + 1 < NSTEP:
                    emit_M(i + 1)
                    emit_S(i + 1)
        elif skew == 4:
            emit_P(0)
            emit_M(0)
            emit_S(0)
            for i in range(NSTEP):
                if i + 1 < NSTEP:
                    emit_P(i + 1)
                emit_O(i)
                if i + 1 < NSTEP:
                    emit_M(i + 1)
                emit_E(i)
                emit_U(i)
                if i + 1 < NSTEP:
                    emit_S(i + 1)
        elif skew == 3:
            emit_S(0)
            for i in range(NSTEP):
                emit_P(i); emit_M(i); emit_O(i)
                if i + 1 < NSTEP:
                    emit_S(i + 1)
                emit_E(i); emit_U(i)
        else:
            emit_P(0)
            emit_M(0)
            emit_S(0)
            for i in range(NSTEP):
                if i + 1 < NSTEP:
                    emit_P(i + 1)
                emit_O(i)
                if i + 1 < NSTEP:
                    emit_M(i + 1)
                emit_U(i)
                emit_E(i)
                if i + 1 < NSTEP:
                    emit_S(i + 1)

    def head_finish(ws, Oacc, gate, gkey, nbc_ap, row0, ybuf, yT):
        sq, ss, rs = ws['sq'], ws['ss'], ws['rs']
        okeys = [('O', c) for c in range(32)]
        TT(sq, Oacc, Oacc, ALU.mult, okeys, ['hsq'])
        S.op('dve', lambda e: e.tensor_reduce(out=ss, in_=sq, axis=AX.X, op=ALU.add), r=['hsq'], w=['hss'])
        ACT(ss, ss, AF.Sqrt, ['hss', 'epsc'], ['hss'], bias=epsc[:, 0:1], scale=1.0 / 128.0)
        RECIP(rs, ss, ['hss'], ['hrs'])
        TT(sq, Oacc, rs.unsqueeze(2).to_broadcast([128, NT, 128]), ALU.mult, okeys + ['hrs'], ['hsq'])
        TT(sq, sq, nbc_ap.unsqueeze(1).to_broadcast([128, NT, 128]), ALU.mult, ['hsq', 'nbc'], ['hsq'])
        TT(ybuf, sq, gate, ALU.mult, ['hsq', gkey], ['ybuf'])
        for g in range(4):
            b = pctr[0] % 4; pctr[0] += 1
            pb = PS[b].bitcast(BF16)
            for j in range(4):
                tt = g * 4 + j
                TR(pb[:, j * 128:(j + 1) * 128], ybuf[:, tt, :], identb, ['ybuf', 'identb'], [('ps', b)])
            CP(yT[:, g * 512:(g + 1) * 512], pb[:, 0:512], [('ps', b)], ['yT'])
        DMA('sp', MIXT[row0:row0 + 128, :], yT, ['yT'], [('MIXT', row0 // 128)], 'yT')

    def mixer_lru(l):
        m = A.mark()
        W = W_IN[l]
        msk = A.alloc([5, T], BF16)
        DMA('pool', msk, MASKS[0:5, :].partition_broadcast(128), [], ['msk'], 'msk')
        cw = A.alloc([4, 4], F32); cb = A.alloc([4], F32); gb = A.alloc([2, 2, 4], F32); lam = A.alloc([2, 4], F32)
        h0 = A.alloc([2, 4], F32)
        DMA('sp', cw, LRU_CWT[:, l], [], ['cw'], 'cw')
        DMA('sp', cb, LRU_CBT[:, l], [], ['cb'], 'cb')
        DMA('sp', gb, LRU_GBT[:, l], [], ['gb'], 'gb')
        DMA('sp', lam, LRU_LAMT[:, l], [], ['lam'], 'lam')
        DMA('sp', h0, S_LRU0T[:, l], [], ['h0'], 'h0')
        nsp = A.alloc([2, 4], F32)
        ACT(nsp, lam, AF.Exp, ['lam'], ['nsp'], scale=-1.0)
        ACT(nsp, nsp, AF.Ln, ['nsp'], ['nsp'], bias=1.0)
        TS(nsp, nsp, -8.0, ALU.mult, ['nsp'], ['nsp'])
        gw = A.alloc([16, 128], BF16)
        DMA('pool', gw, LRU_GW[l].rearrange("d g n c e -> c (d g n) e"), [], ['gw'], 'gw')
        B0 = A.alloc([T], F32); B1 = A.alloc([T], F32); xc = A.alloc([T], F32); B3 = A.alloc([T], F32)
        i_ = A.alloc([T], F32); hs0 = A.alloc([T], F32); hs1 = A.alloc([T], F32); gg = A.alloc([T], F32)
        xcb = A.alloc([T], BF16); yb = A.alloc([T], BF16)
        ho = A.alloc([2, 8], F32)
        xb, a_, xm, u_, r_, g3 = B0, B0, B1, B1, B3, B3
        for n in range(4):
            if l == 0:
                mod_groups(5)

            def ev_x(tq, ps, pk):
                CP(xb[:, tq * 512:(tq + 1) * 512], ps, [pk], ['B0'])
            proj_fm(W, OFF['b_x'] + n * 128, 128, ev_x)

            def ev_g(tq, ps, pk):
                CP(gg[:, tq * 512:(tq + 1) * 512], ps, [pk], ['gg'])
            proj_fm(W, OFF['b_g'] + n * 128, 128, ev_g)
            TS(xc, xb, cw[:, 2, n:n + 1], ALU.mult, ['B0', 'cw', 'cb'], ['xc'], s2=cb[:, n:n + 1], op1=ALU.add)
            TT(xm, xb, msk[:, 0, :], ALU.mult, ['B0', 'msk'], ['B1'])
            STT(xc[:, 2:T], xm[:, 0:T - 2], cw[:, 0, n:n + 1], xc[:, 2:T], ALU.mult, ALU.add, ['B1', 'xc', 'cw'], ['xc'])
            TT(xm, xb, msk[:, 1, :], ALU.mult, ['B0', 'msk'], ['B1'])
            STT(xc[:, 1:T], xm[:, 0:T - 1], cw[:, 1, n:n + 1], xc[:, 1:T], ALU.mult, ALU.add, ['B1', 'xc', 'cw'], ['xc'])
            TT(xm, xb, msk[:, 2, :], ALU.mult, ['B0', 'msk'], ['B1'])
            STT(xc[:, 0:T - 1], xm[:, 1:T], cw[:, 3, n:n + 1], xc[:, 0:T - 1], ALU.mult, ALU.add, ['B1', 'xc', 'cw'], ['xc'])
            ACT(xcb, xc, AF.Copy, ['xc'], ['xcb'])
            for d in range(2):
                for g in range(2):
                    dst, dk_ = (r_, 'B3') if g == 0 else (i_, 'i_')
                    for tq in range(4):
                        b = pctr[0] % 4; pctr[0] += 1
                        MM(PS[b], gw[:, (d * 2 + g) * 4 + n, :], xcb[:, tq * 512:(tq + 1) * 512], True, True,
                           ['gw', 'xcb'], [('ps', b)])
                        ACT(dst[:, tq * 512:(tq + 1) * 512], PS[b], AF.Sigmoid, [('ps', b), 'gb'], [dk_],
                            bias=gb[:, d, g, n:n + 1])
                ACT(a_, r_, AF.Exp, ['B3', 'nsp'], ['B0'], scale=nsp[:, d, n:n + 1])
                TT(u_, a_, a_, ALU.mult, ['B0'], ['B1'])
                ACT(u_, u_, AF.Sqrt, ['B1'], ['B1'], bias=1.0, scale=-1.0)
                TT(u_, u_, i_, ALU.mult, ['B1', 'i_'], ['B1'])
                TT(u_, u_, xc, ALU.mult, ['B1', 'xc'], ['B1'])
                TT(a_, a_, msk[:, 3 + d, :], ALU.mult, ['B0', 'msk'], ['B0'])
                if d == 0:
                    S.op('dve', lambda e, n=n: e.tensor_tensor_scan(out=hs0, data0=a_, data1=u_, initial=h0[:, 0, n:n + 1],
                                                                   op0=ALU.mult, op1=ALU.add), r=['B0', 'B1', 'h0'], w=['hs0'])
                    CPV(ho[:, 0, :], hs0[:, 255:T:256], ['hs0'], [('ho', 0)])
                else:
                    S.op('dve', lambda e, n=n: e.tensor_tensor_scan(out=hs1[:, ::-1], data0=a_[:, ::-1], data1=u_[:, ::-1],
                                                                   initial=h0[:, 1, n:n + 1], op0=ALU.mult, op1=ALU.add),
                         r=['B0', 'B1', 'h0'], w=['hs1'])
                    CPV(ho[:, 1, :], hs1[:, 0:T:256], ['hs1'], [('ho', 1)])
            for d in range(2):
                DMA('sp', O_LRU[:, l, d, n * 128:(n + 1) * 128].rearrange("s p -> p s"), ho[:, d, :], [('ho', d)], [], ('ho', d),
                    is_out=True, slow=True)
            TT(hs0, hs0, hs1, ALU.add, ['hs0', 'hs1'], ['hs0'])
            TT(g3, gg, gg, ALU.mult, ['gg'], ['B3'])
            TS(g3, g3, 0.044715, ALU.mult, ['B3'], ['B3'], s2=1.0, op1=ALU.add)
            TT(g3, g3, gg, ALU.mult, ['B3', 'gg'], ['B3'])
            ACT(g3, g3, AF.Sigmoid, ['B3'], ['B3'], scale=1.5957691216057308)
            TT(g3, g3, gg, ALU.mult, ['B3', 'gg'], ['B3'])
            TT(yb, hs0, g3, ALU.mult, ['hs0', 'B3'], ['yb'])
            DMA('sp', MIXT[512 + n * 128:512 + (n + 1) * 128, :], yb, ['yb'], [('MIXT', 4 + n)], 'yb')
        S.barrier()
        A.release(m)

    def mixer_ret(l):
        m = A.mark()
        W = W_IN[l]
        nbc = A.alloc([512], F32)
        DMA('sp', nbc, RET_NBC[:, l, :], [], ['nbc'], 'nbc')
        dl = A.alloc([8], F32)
        DMA('sp', dl, RET_DLBC[:, l * 8:(l + 1) * 8], [], ['dl'], 'dl')
        ACT(dl, dl, AF.Exp, ['dl'], ['dl'], scale=-1.0)
        ACT(dl, dl, AF.Ln, ['dl'], ['dl'], bias=1.0)
        TS(dl, dl, -1.0, ALU.mult, ['dl'], ['dl'])
        pos = cst[:, 13, :]
        ws = scan_ws()
        OB = [dict(qT=A.alloc([T], BF16), kT=A.alloc([T], BF16), KE=[A.alloc([NT, 128], BF16) for _ in range(2)],
                   V=A.alloc([NT, 128], BF16), G=A.alloc([NT, 128], BF16), pm=[A.alloc([128], F32) for _ in range(2)],
                   cols=A.alloc([2, 8], F32)) for _ in range(2)]
        Oacc = A.alloc([NT, 128], F32)
        ybuf = A.alloc([NT, 128], BF16); yT = A.alloc([T], BF16)
        stg = [A.alloc([128], F32) for _ in range(4)]
        sgc = [0]

        def prep_gen(h):
            o = OB[h % 2]; x = h % 2
            cols, pm = o['cols'], o['pm']
            if l == 0:
                mod_groups(5)
            for d in range(2):
                lg = dl[:, d * 4 + h:d * 4 + h + 1]
                pxi = pos[:, 0:1] if d == 0 else pos[:, 1:2]
                pze = pos[:, 2:3] if d == 0 else pos[:, 3:4]
                ACT(cols[:, d, 0:1], pxi, AF.Exp, ['cst', 'dl'], [('cols', x)], scale=lg)
                ACT(cols[:, d, 1:2], pze, AF.Exp, ['cst', 'dl'], [('cols', x)], scale=lg)
                TS(cols[:, d, 1:2], cols[:, d, 1:2], 128.0 ** -0.5, ALU.mult, [('cols', x)], [('cols', x)])
                ACT(cols[:, d, 2:3], lg, AF.Exp, ['dl'], [('cols', x)], scale=64.0)
                RECIP(cols[:, d, 3:4], cols[:, d, 0:1], [('cols', x)], [('cols', x)])
                TS(pm[d], cst[:, 2 + d, :], cols[:, d, 3:4], ALU.mult, ['cst', ('cols', x)], [('pm', x)])
            yield

            def ev_q(tq, ps, pk):
                CP(o['qT'][:, tq * 512:(tq + 1) * 512], ps, [pk], [('qT', x)])
            yield from proj_fm_g(W, OFF['c_q'] + h * 128, 128, ev_q)

            def ev_k(tq, ps, pk):
                ACT(o['kT'][:, tq * 512:(tq + 1) * 512], ps, AF.Identity, [pk], [('kT', x)], scale=128.0 ** -0.5)
            yield from proj_fm_g(W, OFF['c_k'] + h * 128, 128, ev_k)

            def ev_ktm(tt, ps, pk):
                TS(o['KE'][0][:, tt, :], ps, cols[:, 0, 1:2], ALU.mult, [pk, ('cols', x)], [('KE', x)])
                ACT(o['KE'][1][:, tt, :], ps, AF.Identity, [pk, ('cols', x)], [('KE', x)], scale=cols[:, 1, 1:2])
            yield from proj_tm_g(W, OFF['c_k'] + h * 128, 128, ev_ktm)

            def ev_v(tt, ps, pk):
                CP(o['V'][:, tt, :], ps, [pk], [('V', x)])
            yield from proj_tm_g(W, OFF['c_v'] + h * 128, 128, ev_v)

            def ev_g(tt, ps, pk):
                ACT(o['G'][:, tt, :], ps, AF.Silu, [pk], [('G', x)])
            yield from proj_tm_g(W, OFF['c_g'] + h * 128, 128, ev_g)

        gen = prep_gen(0)
        for _ in gen:
            pass
        for h in range(4):
            o = OB[h % 2]; x = h % 2
            nxt = prep_gen(h + 1) if h < 3 else None

            def init_state(d, s32, key):
                DMA('sp', s32, S_RET0[l, d, h], [], [key], ('S0', d))

            def out_state(d, seg, s32, key):
                i = sgc[0] % 4; sgc[0] += 1
                ACT(stg[i], s32, AF.Copy, [key], [('stg', i)])
                DMA('sp', O_RET[seg, l, d, h], stg[i], [('stg', i)], [], ('stg', i), is_out=True)

            cols = o['cols']
            scan_core(ws, 128, 0, [o['kT'], o['kT']], [o['qT'], o['qT']], o['V'], 128, o['KE'], o['pm'], None,
                      [cols[:, 0, 0:1], cols[:, 1, 0:1]], None, lambda d, c: cols[:, d, 2:3], init_state, out_state, Oacc,
                      [('qT', x), ('kT', x), ('KE', x), ('V', x), ('cols', x), ('pm', x)],
                      bg=(lambda: next(nxt, None)) if (nxt is not None and OVERLAP) else None)
            if nxt is not None:
                for _ in nxt:
                    pass
            head_finish(ws, Oacc, o['G'], ('G', x), nbc[:, h * 128:(h + 1) * 128], 1024 + h * 128, ybuf, yT)
        S.barrier()
        A.release(m)

    def mixer_mlstm(l):
        m = A.mark()
        W = W_IN[l]
        nbc = A.alloc([512], F32)
        DMA('sp', nbc, ML_NBC[:, l, :], [], ['nbc'], 'nbc')
        gbb = A.alloc([16], F32)
        DMA('sp', gbb, ML_GBBC[:, l, :], [], ['gbb'], 'gbb')
        em0 = A.alloc([8], F32)
        DMA('sp', em0, S_MM0BC[:, l * 8:(l + 1) * 8], [], ['em0'], 'em0')
        ACT(em0, em0, AF.Exp, ['em0'], ['em0'])
        n0 = A.alloc([2, 4], F32)
        DMA('sp', n0, S_MN0T[:, l], [], ['n0'], 'n0')
        ift = A.alloc([NT, 16], F32)
        wt, wkey = load_w(W, OFF['d_if'], 16)
        for tt in range(NT):
            for kc in range(KC):
                MM(PS[4][:, tt * 16:(tt + 1) * 16], hT[:, kc, tt * 128:(tt + 1) * 128], wt[:, kc, 0:16], kc == 0, kc == KC - 1,
                   [wkey, ('hT', tt // 4)], [('ps', 4)])
        TT(ift, PS[4][:, 0:256].rearrange("p (a b) -> p a b", a=NT), gbb.unsqueeze(1).to_broadcast([128, NT, 16]), ALU.add,
           [('ps', 4), 'gbb'], ['ift'])
        iv = ift.rearrange("p t (d g h) -> p t d g h", d=2, g=2)
        logi = A.alloc([NT, 2, 4], F32); logf = A.alloc([NT, 2, 4], F32)
        for d in range(2):
            CPV(logi[:, :, d, :], iv[:, :, d, 0, :], ['ift'], ['logi'])
            ACT(logf[:, :, d, :], iv[:, :, d, 1, :], AF.Exp, ['ift'], ['logf'], scale=-1.0)
        ACT(logf, logf, AF.Ln, ['logf'], ['logf'], bias=1.0)
        TS(logf, logf, -1.0, ALU.mult, ['logf'], ['logf'])
        pG = PS[5].rearrange("p (a b) -> p a b", a=NT)
        lf = logf.rearrange("p t d h -> p t (d h)")
        for tt in range(NT):
            MM(pG[:, tt, 0:4], cst[:, 2, :], lf[:, tt, 0:4], True, True, ['cst', 'logf'], [('ps', 5)])
            MM(pG[:, tt, 4:8], cst[:, 3, :], lf[:, tt, 4:8], True, True, ['cst', 'logf'], [('ps', 5)])
            MM(pG[:, tt, 8:16], cst[:, 8, :], lf[:, tt, :], True, True, ['cst', 'logf'], [('ps', 5)])
            MM(pG[:, tt, 16:24], cst[:, 6, :], lf[:, tt, :], True, True, ['cst', 'logf'], [('ps', 5)])
            MM(pG[:, tt, 24:32], cst[:, 7, :], lf[:, tt, :], True, True, ['cst', 'logf'], [('ps', 5)])
        Gs = A.alloc([NT, 32], F32)
        CPV(Gs, pG, [('ps', 5)], ['Gs'])
        li = logi.rearrange("p t d h -> p t (d h)")
        e_ = A.alloc([NT, 8], F32); rsc = A.alloc([NT, 8], F32); emf = A.alloc([NT, 8], F32); kes = A.alloc([NT, 8], F32)
        Ft = A.alloc([NT, 8], F32)
        edec = A.alloc([NT, 2, 8], F32)
        TT(e_, li, Gs[:, :, 0:8], ALU.subtract, ['logi', 'Gs'], ['e_'])
        ACT(rsc, e_, AF.Exp, ['e_'], ['rsc'])
        ACT(emf, Gs[:, :, 0:8], AF.Exp, ['Gs'], ['emf'], scale=-1.0)
        TT(kes, e_, Gs[:, :, 8:16], ALU.add, ['e_', 'Gs'], ['kes'])
        ACT(kes, kes, AF.Exp, ['kes'], ['kes'])
        TS(kes, kes, 128.0 ** -0.5, ALU.mult, ['kes'], ['kes'])
        ACT(edec, Gs[:, :, 16:32].rearrange("p t (a b) -> p t a b", a=2), AF.Exp, ['Gs'], ['edec'])
        CPV(Ft, Gs[:, :, 8:16], ['Gs'], ['Ft'])
        ident32 = cst[:, 0, :]
        pT = PS[6]
        TR(pT[:, 0:128], e_.rearrange("p t k -> p (t k)"), ident32, ['e_', 'cst'], [('ps', 6)])
        TR(pT[:, 128:256], Ft.rearrange("p t k -> p (t k)"), ident32, ['Ft', 'cst'], [('ps', 6)])
        GF = A.alloc([4], F32)
        S.op('dve', lambda e: e.tensor_reduce(out=GF[:, 0:2], in_=pT[:, 0:128].rearrange("p (a b) -> p a b", a=2), axis=AX.X,
                                              op=ALU.max), r=[('ps', 6)], w=['GF'])
        CPV(GF[:, 2:4], pT[:, 128:256:64], [('ps', 6)], ['GF'])
        Xg = A.alloc([2, NT, 2], F32)
        tmask = cst[:, 18, 16:32]
        for q in range(2):
            TT(Xg[:, q], tmask.unsqueeze(2).to_broadcast([128, NT, 2]),
               GF[:, 2 * q:2 * q + 2].unsqueeze(1).to_broadcast([128, NT, 2]), ALU.mult, ['GF', 'cst'], ['Xg'])
        GM = A.alloc([2, 2, 32], F32)
        for d in range(2):
            for q in range(2):
                MM(PS[7][0:4, (d * 2 + q) * 32:(d * 2 + q + 1) * 32], cst[:, 18, d * 4:d * 4 + 4],
                   Xg[:, q].rearrange("p t f -> p (t f)"), True, True, ['Xg', 'cst'], [('ps', 7)])
        CPV(GM[0:4].rearrange("p a b c -> p (a b c)"), PS[7][0:4, 0:128], [('ps', 7)], ['GM'])
        mf = A.alloc([2, 8], F32); mt = A.alloc([8], F32)
        for d in range(2):
            Gv = GM[0:4, d, 0, :].rearrange("p (s k) -> p s k", k=4)
            Fv = GM[0:4, d, 1, :].rearrange("p (s k) -> p s k", k=4)
            order = [0, 1, 2, 3] if d == 0 else [3, 2, 1, 0]
            for i, k in enumerate(order):
                if i == 0:
                    TS(mt[0:4], Gv[:, :, k], 0.0, ALU.max, ['GM'], ['mt'])
                else:
                    TT(mt[0:4], mf[0:4, d, :], Gv[:, :, k], ALU.max, ['mf', 'GM'], ['mt'])
                TT(mf[0:4, d, :], mt[0:4], Fv[:, :, k], ALU.add, ['mt', 'GM'], ['mf'])
            DMA('sp', O_MM[:, l, d, :].rearrange("s h -> h s"), mf[0:4, d, :], ['mf'], [], ('omm', d), is_out=True, slow=True)
        emfs = A.alloc([2, 8], F32)
        ACT(emfs[0:4], mf[0:4], AF.Exp, ['mf'], ['emfs'], scale=-1.0)
        ebc = A.alloc([2, 4, 8], F32)
        for d in range(2):
            for h in range(4):
                MM(PS[7][:, 256 + (d * 4 + h) * 8:256 + (d * 4 + h + 1) * 8], cst[0:4, 14 + h, :], emfs[0:4, d, :], True, True,
                   ['emfs', 'cst'], [('ps', 7)])
        CPV(ebc.rearrange("p a b c -> p (a b c)"), PS[7][:, 256:320], [('ps', 7)], ['ebc'])
        S.barrier()
        ws = scan_ws()
        OB = [dict(qT=A.alloc([T], BF16), kT=A.alloc([T], BF16), KE=[A.alloc([NT, 128], BF16) for _ in range(2)],
                   V=A.alloc([NT, 136], BF16), G=A.alloc([NT, 128], BF16)) for _ in range(2)]
        Oacc = A.alloc([NT, 128], F32)
        ybuf = A.alloc([NT, 128], BF16); yT = A.alloc([T], BF16)
        stg = [A.alloc([136], F32) for _ in range(4)]
        sgc = [0]
        for x in range(2):
            MEMSET(OB[x]['V'][:, :, 128:129], 1.0, [('V', x)])

        def prep_gen(h):
            o = OB[h % 2]; x = h % 2
            if l == 0:
                mod_groups(5)

            def ev_q(tq, ps, pk):
                CP(o['qT'][:, tq * 512:(tq + 1) * 512], ps, [pk], [('qT', x)])
            yield from proj_fm_g(W, OFF['d_q'] + h * 128, 128, ev_q)

            def ev_k(tq, ps, pk):
                ACT(o['kT'][:, tq * 512:(tq + 1) * 512], ps, AF.Identity, [pk], [('kT', x)], scale=128.0 ** -0.5)
            yield from proj_fm_g(W, OFF['d_k'] + h * 128, 128, ev_k)

            def ev_ktm(tt, ps, pk):
                TS(o['KE'][0][:, tt, :], ps, kes[:, tt, h:h + 1], ALU.mult, [pk], [('KE', x)])
                ACT(o['KE'][1][:, tt, :], ps, AF.Identity, [pk], [('KE', x)], scale=kes[:, tt, 4 + h:5 + h])
            yield from proj_tm_g(W, OFF['d_k'] + h * 128, 128, ev_ktm)

            def ev_v(tt, ps, pk):
                CP(o['V'][:, tt, 0:128], ps, [pk], [('V', x)])
            yield from proj_tm_g(W, OFF['d_v'] + h * 128, 128, ev_v)

            def ev_g(tt, ps, pk):
                ACT(o['G'][:, tt, :], ps, AF.Sigmoid, [pk], [('G', x)])
            yield from proj_tm_g(W, OFF['d_o'] + h * 128, 128, ev_g)

        for _ in prep_gen(0):
            pass
        for h in range(4):
            o = OB[h % 2]; x = h % 2
            nxt = prep_gen(h + 1) if h < 3 else None

            def init_state(d, s32, key):
                DMA('sp', s32[:, 0:128], S_MC0[l, d, h], [], [key], ('S0', d))
                CPV(s32[:, 128:129], n0[:, d, h:h + 1], [key], [key])
                TS(s32, s32, em0[:, d * 4 + h:d * 4 + h + 1], ALU.mult, [key], [key])

            def out_state(d, seg, s32, key):
                i = sgc[0] % 4; sgc[0] += 1
                ACT(stg[i][:, 0:129], s32, AF.Identity, [key], [('stg', i)], scale=ebc[:, d, h, seg:seg + 1])
                DMA('sp', O_MC[seg, l, d, h], stg[i][:, 0:128], [('stg', i)], [], ('stg', i), is_out=True)
                DMA('sp', O_MN[seg, l, d, h].rearrange("(p o) -> p o", o=1), stg[i][:, 128:129], [('stg', i)], [], ('stgn', i),
                    is_out=True)

            scan_core(ws, 128, 0, [o['kT'], o['kT']], [o['qT'], o['qT']], o['V'], 129, o['KE'], [cst[:, 2, :], cst[:, 3, :]],
                      [rsc[:, :, h], rsc[:, :, 4 + h]], None, [emf[:, :, h], emf[:, :, 4 + h]],
                      lambda d, c: edec[:, c // 2, c % 2, d * 4 + h:d * 4 + h + 1], init_state, out_state, Oacc,
                      [('qT', x), ('kT', x), ('KE', x), ('V', x)],
                      bg=(lambda: next(nxt, None)) if (nxt is not None and OVERLAP) else None)
            if nxt is not None:
                for _ in nxt:
                    pass
            head_finish(ws, Oacc, o['G'], ('G', x), nbc[:, h * 128:(h + 1) * 128], 1536 + h * 128, ybuf, yT)
        S.barrier()
        A.release(m)

    def mixer_gla(l):
        m = A.mark()
        W = W_IN[l]
        nbc = A.alloc([512], F32)
        DMA('sp', nbc, GLA_NBC[:, l, :], [], ['nbc'], 'nbc')
        upb = A.alloc([2, 256], BF16)
        DMA('pool', upb[0:64], GLA_UPB[l].rearrange("d r c -> r d c"), [], ['upb'], 'upb')
        lr1 = [A.alloc([T], BF16) for _ in range(2)]
        for d in range(2):
            MEMSET(lr1[d][0:64, :], 1.0, [('lr1', d)])

            def ev_lr(tq, ps, pk, d=d):
                CP(lr1[d][0:16, tq * 512:(tq + 1) * 512], ps, [pk], [('lr1', d)])
            proj_fm(W, OFF['a_lr'] + d * 16, 16, ev_lr)
        ws = scan_ws()
        qT = A.alloc([T], BF16); kT = A.alloc([T], BF16); ktm = A.alloc([NT, 128], BF16)
        KS = [A.alloc([T], BF16) for _ in range(2)]
        QS = [[A.alloc([T], BF16) for _ in range(2)] for _ in range(2)]
        for d in range(2):
            for hh in range(2):
                MEMSET(QS[d][hh], 0.0, ['QS'])
        MEMSET(ws['S16'], 0.0, [('S16', 0), ('S16', 1)])
        KE = [A.alloc([NT, 128], BF16) for _ in range(2)]
        V = A.alloc([NT, 256], BF16); G = A.alloc([NT, 256], BF16)
        eend = A.alloc([2, 32], F32)
        spt = [A.alloc([256], F32) for _ in range(2)]
        ebm = [A.alloc([128], F32) for _ in range(2)]
        eq = [A.alloc([128], F32) for _ in range(2)]
        Oacc = A.alloc([NT, 128], F32)
        ybuf = A.alloc([NT, 128], BF16); yT = A.alloc([T], BF16)
        stg = [A.alloc([128], F32) for _ in range(4)]
        sgc = [0]
        for hp in range(2):
            if l == 0:
                mod_groups(10)

            def ev_q(tq, ps, pk):
                ACT(qT[:, tq * 512:(tq + 1) * 512], ps, AF.Identity, [pk], ['qT'], scale=0.125)
            proj_fm(W, OFF['a_q'] + hp * 128, 128, ev_q)

            def ev_k(tq, ps, pk):
                CP(kT[:, tq * 512:(tq + 1) * 512], ps, [pk], ['kT'])
            proj_fm(W, OFF['a_k'] + hp * 128, 128, ev_k)

            def ev_ktm(tt, ps, pk):
                CP(ktm[:, tt, :], ps, [pk], ['ktm'])
            proj_tm(W, OFF['a_k'] + hp * 128, 128, ev_ktm)

            def ev_v(tt, ps, pk):
                CP(V[:, tt, :], ps, [pk], ['V'])
            proj_tm(W, OFF['a_v'] + hp * 256, 256, ev_v)

            def ev_g(tt, ps, pk):
                ACT(G[:, tt, :], ps, AF.Silu, [pk], ['G'])
            proj_tm(W, OFF['a_g'] + hp * 256, 256, ev_g)
            if getattr(build, 'gla_stop', None) == 'A':
                continue
            for tt in range(NT):
                tsl = slice(tt * 128, (tt + 1) * 128)
                for d in range(2):
                    s = d
                    b = pctr[0] % 4; pctr[0] += 1
                    MM(PS[b][:, 0:256], lr1[d][0:64, tsl], upb[0:64, d, :], True, True, [('lr1', d), 'upb'], [('ps', b)])
                    ACT(spt[s], PS[b][:, 0:256], AF.Exp, [('ps', b)], [('spt', s)], scale=-1.0)
                    ACT(spt[s], spt[s], AF.Ln, [('spt', s)], [('spt', s)], bias=1.0)
                    sph = spt[s][:, hp * 128:(hp + 1) * 128]
                    b2 = pctr[0] % 4; pctr[0] += 1
                    MM(PS[b2][:, 0:128], cst[:, 11 + d, :], sph, True, True, [('spt', s), 'cst'], [('ps', b2)])
                    ACT(ebm[s], PS[b2][:, 0:128], AF.Exp, [('ps', b2)], [('ebm', s)])
                    TT(KE[d][:, tt, :], ktm[:, tt, :], ebm[s], ALU.mult, [('ebm', s), 'ktm'], ['KE'])
                    b3 = pctr[0] % 4; pctr[0] += 1
                    MM(PS[b3][:, 0:128], sph, cst[:, 9 + d, :], True, True, [('spt', s), 'cst'], [('ps', b3)])
                    ACT(eq[s], PS[b3][:, 0:128], AF.Exp, [('ps', b3)], [('eq', s)])
                    for hh in range(2):
                        rr = slice(hh * 64, hh * 64 + 64)
                        TT(QS[d][hh][rr, tsl], qT[rr, tsl], eq[s][rr, :], ALU.mult, [('eq', s), 'qT'], ['QS'])
                    if d == 0:
                        CPV(eend[:, d, 2 * tt:2 * tt + 2], eq[s][:, 63:128:64], [('eq', s)], ['eend'])
                    else:
                        CPV(eend[:, d, 2 * tt:2 * tt + 2], eq[s][:, 0:128:64], [('eq', s)], ['eend'])
                    ACT(ebm[s], PS[b3][:, 0:128], AF.Exp, [('ps', b3)], [('ebm', s)], scale=-1.0)
                    TT(KS[d][:, tsl], kT[:, tsl], ebm[s], ALU.mult, [('ebm', s), 'kT'], ['KS'])
            if getattr(build, 'gla_stop', None) == 'B':
                continue
            for hh in range(2):
                if getattr(build, 'gla_stop', None) == 'C' and hh == 1:
                    continue
                h = hp * 2 + hh
                base = hh * 64

                def init_state(d, s32, key):
                    DMA('sp', s32, S_GLA0[l, d, h], [], [key], ('S0', d))

                def out_state(d, seg, s32, key):
                    i = sgc[0] % 4; sgc[0] += 1
                    ACT(stg[i][base:base + 64, :], s32, AF.Copy, [key], [('stg', i)])
                    DMA('sp', O_GLA[seg, l, d, h], stg[i][base:base + 64, :], [('stg', i)], [], ('stg', i), is_out=True)

                scan_core(ws, 64, base, KS, [QS[0][hh], QS[1][hh]], V[:, :, hh * 128:(hh + 1) * 128], 128,
                          [KE[0][:, :, hh * 64:(hh + 1) * 64], KE[1][:, :, hh * 64:(hh + 1) * 64]],
                          [cst[:, 2, :], cst[:, 3, :]], None, None, None,
                          lambda d, c: eend[base:base + 64, d, c:c + 1], init_state, out_state, Oacc,
                          ['QS', 'KS', 'KE', 'V', 'eend'], full_k=True, skew=0)
                head_finish(ws, Oacc, G[:, :, hh * 128:(hh + 1) * 128], 'G', nbc[:, h * 128:(h + 1) * 128], h * 128, ybuf, yT)
        S.barrier()
        A.release(m)

    def phase_wout(l, XS, XD):
        m = A.mark()
        mx = A.alloc([KC, T], BF16)
        for tq in range(4):
            DMA('sp', mx[:, :, tq * 512:(tq + 1) * 512], XTv(MIXT)[:, :, tq * 512:(tq + 1) * 512], [], [('mx', tq)], ('mx', tq))
        NX = 6
        xa = [A.alloc([512], F32) for _ in range(NX)]
        groups = [(oc, tq) for oc in range(KC) for tq in range(4)]

        def load_x(gi):
            oc, tq = groups[gi]
            s = gi % NX
            DMA('sp', xa[s], XS[oc * 128:(oc + 1) * 128, tq * 512:(tq + 1) * 512], [('XS', oc, tq)], [('xa', s)], ('xa', s))

        for gi in range(3):
            load_x(gi)
        wt = wkey = None
        for gi, (oc, tq) in enumerate(groups):
            if tq == 0:
                wt, wkey = load_w(W_OUT[l], oc * 128, 128)
            if gi + 3 < len(groups):
                load_x(gi + 3)
            b = gi % 8
            s = gi % NX
            for kc in range(KC):
                MM(PS[b], wt[:, kc, 0:128], mx[:, kc, tq * 512:(tq + 1) * 512], kc == 0, kc == KC - 1, [wkey, ('mx', tq)], [('ps', b)])
            STT(xa[s], PS[b], AB[:, l, 2, oc:oc + 1], xa[s], ALU.mult, ALU.add, [('ps', b), ('xa', s), 'AB'], [('xa', s)])
            DMA('sp', XD[oc * 128:(oc + 1) * 128, tq * 512:(tq + 1) * 512], xa[s], [('xa', s)], [('XD', oc, tq)], ('xas', s))
        S.barrier()
        A.release(m)

    def phase_ffn_up(l):
        m = A.mark()
        msk = A.alloc([2, T], BF16)
        DMA('pool', msk, MASKS[5:7, :].partition_broadcast(128), [], ['msk'], 'msk')
        cw = A.alloc([9, FC], F32); cb = A.alloc([FC], F32)
        DMA('sp', cw, FFN_CWT[:, l], [], ['cw'], 'cw')
        DMA('sp', cb, FFN_CBT[:, l], [], ['cb'], 'cb')
        u_ = [A.alloc([T], F32) for _ in range(3)]
        g_ = [A.alloc([T], F32) for _ in range(2)]
        acc = [A.alloc([T], F32) for _ in range(2)]
        gl = A.alloc([T], F32); gr = A.alloc([T], F32)
        ab = [A.alloc([T], BF16) for _ in range(2)]

        def stage_a(cc):
            us, gs = cc % 3, cc % 2
            i = wctr[0] % len(wslots); wctr[0] += 1
            wt = wslots[i]; wkey = ('w', i)
            DMA('pool', wt[:, :, 0:128], W_UP[l][:, cc * 128:(cc + 1) * 128].rearrange("(kc p) n -> p kc n", p=128), [], [wkey], ('w', i))
            DMA('pool', wt[:, :, 128:256], W_UP[l][:, DFF + cc * 128:DFF + (cc + 1) * 128].rearrange("(kc p) n -> p kc n", p=128),
                [], [wkey], ('w', i))

            def ev_u(tq, ps, pk):
                ACT(u_[us][:, tq * 512:(tq + 1) * 512], ps, AF.Copy, [pk], [('u', us)])
            proj_fm(None, 0, 128, ev_u, sub=0, wt=wt, wkey=wkey)

            def ev_g(tq, ps, pk):
                ACT(g_[gs][:, tq * 512:(tq + 1) * 512], ps, AF.Copy, [pk], [('g', gs)])
            proj_fm(None, 0, 128, ev_g, sub=128, wt=wt, wkey=wkey)

        def stage_b0(cc):
            gs, s = cc % 2, cc % 2
            ACT(acc[s], g_[gs], AF.Identity, [('g', gs), 'cw', 'cb'], [('acc', s)], bias=cb[:, cc:cc + 1], scale=cw[:, 4, cc:cc + 1])

        def stage_b(cc):
            gs, s = cc % 2, cc % 2
            TT(gl, g_[gs], msk[:, 0, :], ALU.mult, [('g', gs), 'msk'], ['gl'], eng='pool')
            TT(gr, g_[gs], msk[:, 1, :], ALU.mult, [('g', gs), 'msk'], ['gr'], eng='pool')
            for dc in (0, -1, 1):
                for dr in (-1, 0, 1):
                    if dr == 0 and dc == 0:
                        continue
                    off = 64 * dr + dc
                    lo, hi = max(0, -off), min(T, T - off)
                    src = gl if dc == -1 else (gr if dc == 1 else g_[gs])
                    sk = 'gl' if dc == -1 else ('gr' if dc == 1 else ('g', gs))
                    STT(acc[s][:, lo:hi], src[:, lo + off:hi + off], cw[:, (dr + 1) * 3 + dc + 1, cc:cc + 1], acc[s][:, lo:hi],
                        ALU.mult, ALU.add, [sk, ('acc', s), 'cw'], [('acc', s)])

        def stage_c(cc):
            us, s = cc % 3, cc % 2
            ACT(acc[s], acc[s], AF.Silu, [('acc', s)], [('acc', s)])
            TT(ab[s], acc[s], u_[us], ALU.mult, [('acc', s), ('u', us)], [('ab', s)])
            DMA('sp', ACTS[cc * 128:(cc + 1) * 128, :], ab[s], [('ab', s)], [('ACTS', cc)], ('ab', s))

        for it in range(FC + 2):
            if 0 <= it - 1 < FC:
                stage_b0(it - 1)
            if it < FC:
                stage_a(it)
            if 0 <= it - 1 < FC:
                stage_b(it - 1)
            if 0 <= it - 2 < FC:
                stage_c(it - 2)
        S.barrier()
        A.release(m)

    def phase_ffn_down(l, XS, XD):
        m = A.mark()
        at = A.alloc([FC, 1024], BF16)
        HA = Arena(hT_raw, 16384)
        wd = [HA.alloc([FC, 128], BF16) for _ in range(3)]
        xa = [HA.alloc([512], F32) for _ in range(4)]
        av = ACTS.rearrange("(cc p) t -> p cc t", p=128)
        xi = 0
        wi = 0
        for t2 in range(2):
            for ch in range(4):
                for hf in range(2):
                    DMA('sp', at[:, ch * 11:(ch + 1) * 11, hf * 512:(hf + 1) * 512],
                        av[:, ch * 11:(ch + 1) * 11, t2 * 1024 + hf * 512:t2 * 1024 + (hf + 1) * 512], [],
                        [('at', hf, ch)], ('at', hf, ch))
            for oc in range(KC):
                ws = wi % 3; wi += 1
                DMA('pool', wd[ws], W_DOWN[l][:, oc * 128:(oc + 1) * 128].rearrange("(cc p) n -> p cc n", p=128), [], [('wd', ws)],
                    ('wd', ws))
                for hf in range(2):
                    tq = t2 * 2 + hf
                    b = pctr[0] % 8; pctr[0] += 1
                    xs = xi % 4; xi += 1
                    DMA('sp', xa[xs], XS[oc * 128:(oc + 1) * 128, tq * 512:(tq + 1) * 512], [('XS', oc, tq)], [('xa', xs)], ('xa', xs))
                    for cc in range(FC):
                        MM(PS[b], wd[ws][:, cc, :], at[:, cc, hf * 512:(hf + 1) * 512], cc == 0, cc == FC - 1,
                           [('wd', ws), ('at', hf, cc // 11)], [('ps', b)])
                    STT(xa[xs], PS[b], AB[:, l, 5, oc:oc + 1], xa[xs], ALU.mult, ALU.add, [('ps', b), ('xa', xs), 'AB'], [('xa', xs)])
                    DMA('sp', XD[oc * 128:(oc + 1) * 128, tq * 512:(tq + 1) * 512], xa[xs], [('xa', xs)], [('XD', oc, tq)], ('xas', xs))
        S.barrier()
        A.release(m)

    def dump(name, src):
        if name in DUMP:
            S.barrier()
            DMA('sp', DUMP[name], src, [], [], 'dump_' + name, is_out=True)
            S.barrier()

    def program():
        phase_prologue()
        if stop_after == 'prologue':
            return
        for l in range(DEPTH):
            phase_norm(XTA, AB[:, l, 0, :], AB[:, l, 1, :])
            if l == 0 and 'hT' in DUMP:
                S.barrier()
                DMA('sp', XTv(DUMP['hT']), hT, [], [], 'dump_hT', is_out=True)
                S.barrier()
            if stop_after == 'norm1':
                return
            m = A.mark()
            wslots.clear()
            wslots.extend(A.alloc([KC, 256], BF16) for _ in range(3))
            for mx in mixers:
                {'lru': mixer_lru, 'ret': mixer_ret, 'mlstm': mixer_mlstm, 'gla': mixer_gla}[mx](l)
            if l == 0:
                mod_groups(NMODG)
                mod_finalize(0, [2, 3, 4, 5])
                mod_finalize(1, [0, 1, 2, 3, 4, 5])
                S.barrier()
            A.release(m)
            if l == 0:
                dump('mixT', MIXT)
            if stop_after == 'mix':
                return
            m = A.mark()
            wslots.clear()
            wslots.extend(A.alloc([KC, 256], BF16) for _ in range(3))
            phase_wout(l, XTA, XTB)
            A.release(m)
            if l == 0:
                dump('xtb', XTB)
            if stop_after == 'wout':
                return
            phase_norm(XTB, AB[:, l, 3, :], AB[:, l, 4, :])
            m = A.mark()
            wslots.clear()
            wslots.extend(A.alloc([KC, 256], BF16) for _ in range(3))
            phase_ffn_up(l)
            A.release(m)
            m = A.mark()
            phase_ffn_down(l, XTB, XTA)
            A.release(m)
            if l == 0:
                dump('xta', XTA)
            if stop_after == 'ffn':
                return
        phase_norm(XTA, fnT, None, final=True)

    mixers = build.mixers
    OVERLAP = getattr(build, 'overlap', True)
    program()
    S.finish()
    S.emit(nc, st)
    st.close()
    build.stats = dict(n_ops=S.n, arena_hi=A.hi, per_eng={e: len(S.q[e]) for e in ENGS})
    return nc


build.mixers = ['gla', 'lru', 'ret', 'mlstm']


def make_in_maps(inp):
    f32 = np.float32
    g = lambda k: np.asarray(inp[k], f32)
    consts = make_consts()
    shared = {
        "consts": consts,
        "w_mod": g('w_mod'), "w_in": g('w_in'), "w_out": g('w_out'), "w_up": g('w_up'), "w_down": g('w_down'),
        "b_modT": np.ascontiguousarray(fm(g('b_mod'), 96)),
        "norm1T": fm(g('norm1_w'), KC), "norm2T": fm(g('norm2_w'), KC), "fnormT": fm(g('final_norm_w'), KC),
        "gla_nbc": bc(g('gla_norm_w')).reshape(128, DEPTH, 512),
        "ret_nbc": bc(g('ret_norm_w')).reshape(128, DEPTH, 512),
        "ml_nbc": bc(g('mlstm_norm_w')).reshape(128, DEPTH, 512),
        "lru_cwT": fm(g('lru_conv_w'), 4),
        "lru_cbT": fm(g('lru_conv_b'), 4),
        "lru_gw": g('lru_gate_w'),
        "lru_gbT": fm(g('lru_gate_b'), 4),
        "lru_lamT": fm(g('lru_lambda'), 4),
        "ret_dlbc": bc(g('ret_decay_logit')),
        "ml_gbbc": bc(g('mlstm_gate_b')).reshape(128, DEPTH, 16),
        "ffn_cbT": fm(g('ffn_conv_b'), FC),
    }
    upb = np.zeros((DEPTH, 2, 64, 256), f32)
    upb[:, :, 0:16, :] = g('gla_gate_up')
    upb[:, :, 16, :] = g('gla_gate_b')
    shared["gla_upb"] = upb
    cw = g('ffn_conv_w')
    cw_s = fm(cw.reshape(DEPTH, 9, DFF), FC)
    cwp = np.zeros_like(cw)
    cwp[:, 1] = cw[:, 1]
    cw_p = fm(cwp.reshape(DEPTH, 9, DFF), FC)
    maps = []
    for core in range(8):
        d = dict(shared)
        if core < 4:
            d["x"] = np.ascontiguousarray(g('x_prompt')[core * 8:(core + 1) * 8].reshape(T, D))
            d["condT"] = fm(g('c_ctx'), KC)
            d["keep"] = np.zeros((128, 1), f32)
            d["masks"] = core_masks(256, 256, 0.0)
            d["s_gla0"] = np.zeros((DEPTH, 2, 4, 64, 128), f32)
            d["s_lru0T"] = np.zeros((128, DEPTH, 2, 4), f32)
            d["s_ret0"] = np.zeros((DEPTH, 2, 4, 128, 128), f32)
            d["s_mc0"] = np.zeros((DEPTH, 2, 4, 128, 128), f32)
            d["s_mn0T"] = np.zeros((128, DEPTH, 2, 4), f32)
            d["s_mm0bc"] = np.zeros((128, DEPTH * 8), f32)
            d["ffn_cwT"] = cw_p
        else:
            b = core - 4
            d["x"] = np.ascontiguousarray(g('x_sample')[b])
            d["condT"] = fm(g('c')[b], KC)
            d["keep"] = np.ones((128, 1), f32)
            d["masks"] = core_masks(T, 64, 1.0)
            d["s_gla0"] = np.ascontiguousarray(g('state_gla')[b])
            d["s_lru0T"] = fm(g('state_lru')[b], 4)
            d["s_ret0"] = np.ascontiguousarray(g('state_ret')[b])
            d["s_mc0"] = np.ascontiguousarray(g('state_mlstm_c')[b])
            d["s_mn0T"] = np.ascontiguousarray(np.moveaxis(g('state_mlstm_n')[b], -1, 0))
            d["s_mm0bc"] = bc(g('state_mlstm_m')[b])
            d["ffn_cwT"] = cw_s
        maps.append(d)
    return maps


_NC = {}


def kernel(**inputs):
    if 'nc' not in _NC:
        _NC['nc'] = build()
    nc = _NC['nc']
    maps = make_in_maps(inputs)
    res = run_bass_kernel_spmd(nc, maps, core_ids=list(range(8)))
    R = res.results
    y_prompt = np.concatenate([R[i]["y"].reshape(8, 256, D) for i in range(4)], axis=0)
    y_sample = np.stack([R[4 + i]["y"] for i in range(4)], axis=0)

    def cat(name):
        return np.concatenate([R[i][name] for i in range(4)], axis=0)
    return (y_prompt.astype(np.float32), y_sample.astype(np.float32), cat("o_gla"), cat("o_lru"), cat("o_ret"),
            cat("o_mc"), cat("o_mn"), cat("o_mm"))
```
